# Optimizing a Trainium2 kernel written in Bass

```python
import math
import jax, jax.numpy as jnp
from jax import lax
import numpy as np

D_MODEL = 1024
BATCH = 4
SEQ = 4096
DEPTH = 2

N_A_LAYERS = max(1, DEPTH // 2)
N_B_LAYERS = DEPTH - N_A_LAYERS

A_HEADS = D_MODEL // 128
A_HEAD_DIM = 64
A_V_DIM = 2 * A_HEAD_DIM
A_QK_WIDTH = A_HEADS * 2 * A_HEAD_DIM
A_V_WIDTH = A_HEADS * A_V_DIM
LAMBDA_INIT_BASE = 0.8
LAMBDA_INIT_AMP = 0.6
LAMBDA_INIT_RATE = 0.3

ROPE_THETA = 500000.0
ROT_DIM = A_HEAD_DIM // 4

B_HEADS = D_MODEL // 64
B_HEAD_DIM = 64
B_WIDTH = B_HEADS * B_HEAD_DIM

D_FF = -(-8 * D_MODEL // (3 * 256)) * 256

Q_BLOCK = 128
EPS = 1e-6

kernel_name = 'yoco_diff_stickbreaking_trunk'


def rms_norm(x, g):
    x32 = x.astype(jnp.float32)
    y = x32 * lax.rsqrt(jnp.mean(x32 * x32, axis=-1, keepdims=True) + EPS)
    return (y * g.astype(jnp.float32)).astype(x.dtype)


def rope_tables(seq):
    pos = jnp.arange(seq, dtype=jnp.float32)
    inv_freq = ROPE_THETA ** (-jnp.arange(0, ROT_DIM, 2, dtype=jnp.float32) / ROT_DIM)
    ang = pos[:, None] * inv_freq[None, :]
    return jnp.cos(ang), jnp.sin(ang)


def apply_partial_rope(t, cos, sin):
    half = ROT_DIM // 2
    t1 = t[..., :half].astype(jnp.float32)
    t2 = t[..., half:ROT_DIM].astype(jnp.float32)
    r1 = t1 * cos - t2 * sin
    r2 = t2 * cos + t1 * sin
    return jnp.concatenate([r1.astype(t.dtype), r2.astype(t.dtype), t[..., ROT_DIM:]], axis=-1)


def to_blocks(t):
    b, h, s, d = t.shape
    return t.reshape(b, h, s // Q_BLOCK, Q_BLOCK, d).transpose(2, 0, 1, 3, 4)


def from_blocks(o):
    nb, b, h, qb, d = o.shape
    return o.transpose(1, 0, 3, 2, 4).reshape(b, nb * qb, h, d)


def diff_attention(h, w_qkv, w_o, lq1, lk1, lq2, lk2, subln_g, lambda_init, cos, sin):
    b, s, _ = h.shape
    qkv = h @ w_qkv
    q, k, v = jnp.split(qkv, [A_QK_WIDTH, 2 * A_QK_WIDTH], axis=-1)
    q = q.reshape(b, s, 2 * A_HEADS, A_HEAD_DIM).transpose(0, 2, 1, 3)
    k = k.reshape(b, s, 2 * A_HEADS, A_HEAD_DIM).transpose(0, 2, 1, 3)
    v = v.reshape(b, s, A_HEADS, A_V_DIM).transpose(0, 2, 1, 3)
    q = apply_partial_rope(q, cos, sin)
    k = apply_partial_rope(k, cos, sin)
    f32 = jnp.float32
    lam = (jnp.exp(jnp.sum(lq1.astype(f32) * lk1.astype(f32)))
           - jnp.exp(jnp.sum(lq2.astype(f32) * lk2.astype(f32))) + lambda_init)
    scale = A_HEAD_DIM ** -0.5
    kpos = jnp.arange(s)
    n_blocks = s // Q_BLOCK

    def block(args):
        qb, start = args
        sc = jnp.einsum('bhqd,bhkd->bhqk', qb, k).astype(f32) * scale
        qpos = start + jnp.arange(Q_BLOCK)
        mask = kpos[None, :] <= qpos[:, None]
        p = jax.nn.softmax(jnp.where(mask, sc, -jnp.inf), axis=-1)
        p = p.reshape(b, A_HEADS, 2, Q_BLOCK, s)
        a = p[:, :, 0] - lam * p[:, :, 1]
        return jnp.einsum('bhqk,bhkd->bhqd', a.astype(v.dtype), v)

    o = lax.map(block, (to_blocks(q), jnp.arange(n_blocks) * Q_BLOCK))
    o = from_blocks(o)
    o = rms_norm(o, subln_g) * (1.0 - lambda_init)
    return o.reshape(b, s, A_V_WIDTH).astype(h.dtype) @ w_o


def stick_breaking_attention(h, w_q, w_o, k, v):
    b, s, _ = h.shape
    q = (h @ w_q).reshape(b, s, B_HEADS, B_HEAD_DIM).transpose(0, 2, 1, 3)
    f32 = jnp.float32
    scale = B_HEAD_DIM ** -0.5
    kpos = jnp.arange(s)
    n_blocks = s // Q_BLOCK

    def block(args):
        qb, start = args
        z = jnp.einsum('bhqd,bhkd->bhqk', qb, k).astype(f32) * scale
        qpos = start + jnp.arange(Q_BLOCK)
        mask = kpos[None, :] < qpos[:, None]
        log_beta = jax.nn.log_sigmoid(z)
        log_1m = jnp.where(mask, jax.nn.log_sigmoid(-z), 0.0)
        between = lax.cumsum(log_1m, axis=3, reverse=True) - log_1m
        a = jnp.where(mask, jnp.exp(log_beta + between), 0.0)
        return jnp.einsum('bhqk,bhkd->bhqd', a.astype(v.dtype), v)

    o = lax.map(block, (to_blocks(q), jnp.arange(n_blocks) * Q_BLOCK))
    o = from_blocks(o).reshape(b, s, B_WIDTH)
    return o @ w_o


def swiglu(h, w_gate_up, w_down):
    gate, up = jnp.split(h @ w_gate_up, 2, axis=-1)
    return (jax.nn.silu(gate) * up) @ w_down


def setup_inputs(seed: int = 0) -> dict:
    key = jax.random.key(seed)
    ks = jax.random.split(key, 24)
    f32 = jnp.float32

    def w(k, shape, fan_in):
        return jax.random.normal(k, shape, f32) * (fan_in ** -0.5)

    def gain(k, shape):
        return 1.0 + 0.02 * jax.random.normal(k, shape, f32)

    return {
        'x': jax.random.normal(ks[0], (BATCH, SEQ, D_MODEL), f32),
        'a_w_qkv': w(ks[1], (N_A_LAYERS, D_MODEL, 2 * A_QK_WIDTH + A_V_WIDTH), D_MODEL),
        'a_w_o': w(ks[2], (N_A_LAYERS, A_V_WIDTH, D_MODEL), A_V_WIDTH),
        'a_lambda_q1': 0.1 * jax.random.normal(ks[3], (N_A_LAYERS, A_HEAD_DIM), f32),
        'a_lambda_k1': 0.1 * jax.random.normal(ks[4], (N_A_LAYERS, A_HEAD_DIM), f32),
        'a_lambda_q2': 0.1 * jax.random.normal(ks[5], (N_A_LAYERS, A_HEAD_DIM), f32),
        'a_lambda_k2': 0.1 * jax.random.normal(ks[6], (N_A_LAYERS, A_HEAD_DIM), f32),
        'a_subln_g': gain(ks[7], (N_A_LAYERS, A_V_DIM)),
        'kv_norm_g': gain(ks[8], (D_MODEL,)),
        'kv_w': w(ks[9], (D_MODEL, 2 * B_WIDTH), D_MODEL),
        'b_w_q': w(ks[10], (N_B_LAYERS, D_MODEL, B_WIDTH), D_MODEL),
        'b_w_o': w(ks[11], (N_B_LAYERS, B_WIDTH, D_MODEL), B_WIDTH),
        'mix_pre_g': gain(ks[12], (DEPTH, D_MODEL)),
        'mix_post_g': gain(ks[13], (DEPTH, D_MODEL)),
        'ffn_pre_g': gain(ks[14], (DEPTH, D_MODEL)),
        'ffn_post_g': gain(ks[15], (DEPTH, D_MODEL)),
        'ffn_w_gate_up': w(ks[16], (DEPTH, D_MODEL, 2 * D_FF), D_MODEL),
        'ffn_w_down': w(ks[17], (DEPTH, D_FF, D_MODEL), D_FF),
    }


def reference(x, a_w_qkv, a_w_o, a_lambda_q1, a_lambda_k1, a_lambda_q2, a_lambda_k2,
              a_subln_g, kv_norm_g, kv_w, b_w_q, b_w_o, mix_pre_g, mix_post_g,
              ffn_pre_g, ffn_post_g, ffn_w_gate_up, ffn_w_down):
    b, s, _ = x.shape
    cos, sin = rope_tables(s)
    shared_k = None
    shared_v = None
    for layer in range(DEPTH):
        h = rms_norm(x, mix_pre_g[layer])
        if layer < N_A_LAYERS:
            lambda_init = LAMBDA_INIT_BASE - LAMBDA_INIT_AMP * math.exp(-LAMBDA_INIT_RATE * layer)
            m = diff_attention(h, a_w_qkv[layer], a_w_o[layer],
                               a_lambda_q1[layer], a_lambda_k1[layer],
                               a_lambda_q2[layer], a_lambda_k2[layer],
                               a_subln_g[layer], lambda_init, cos, sin)
        else:
            if shared_k is None:
                kv = rms_norm(x, kv_norm_g) @ kv_w
                kk, vv = jnp.split(kv, 2, axis=-1)
                shared_k = kk.reshape(b, s, B_HEADS, B_HEAD_DIM).transpose(0, 2, 1, 3)
                shared_v = vv.reshape(b, s, B_HEADS, B_HEAD_DIM).transpose(0, 2, 1, 3)
            j = layer - N_A_LAYERS
            m = stick_breaking_attention(h, b_w_q[j], b_w_o[j], shared_k, shared_v)
        x = x + rms_norm(m, mix_post_g[layer])
        f = swiglu(rms_norm(x, ffn_pre_g[layer]), ffn_w_gate_up[layer], ffn_w_down[layer])
        x = x + rms_norm(f, ffn_post_g[layer])
    return x
```

```python
import contextlib
import numpy as np
import concourse.bass as bass
import concourse.mybir as mybir
from concourse.bass_utils import run_bass_kernel_spmd

F32 = mybir.dt.float32
BF16 = mybir.dt.bfloat16
AF = mybir.ActivationFunctionType
ALU = mybir.AluOpType
AX = mybir.AxisListType

S = 4096
D = 1024
TC = 512
NCH = 8
DFF = 2816
NJ = 22
OWN = ([0, 3, 4, 7], [1, 2, 5, 6])
EPS = 1e-6
LAMBDA_INIT = 0.8 - 0.6
G_MIXPRE, G_MIXPOST, G_FFNPRE, G_FFNPOST, G_KV, G_SUB = 0, 16, 32, 48, 64, 72
NG = 80

COMPUTE = ("pe", "act", "dve", "pool")
DBG = {}


class Res:
    __slots__ = ("name", "w", "rs")

    def __init__(self, name=""):
        self.name = name
        self.w = None
        self.rs = []


class Op:
    __slots__ = ("eng", "fn", "deps", "is_dma", "key", "needed", "ms", "idx")


class Prog:
    def __init__(self, nc):
        self.nc = nc
        self.ops = []
        self.dma_issued = {}
        self.dma_inc = {}
        self.last = {}

    def _add(self, eng, fn, reads, writes, is_dma, key, extra_deps=()):
        op = Op()
        op.eng = eng; op.fn = fn; op.is_dma = is_dma; op.key = key
        op.needed = False; op.ms = None; op.idx = len(self.ops)
        deps = {}

        def add_dep(d, cnt=None):
            if d is None or d is op:
                return
            if d.is_dma:
                c = self.dma_issued[d.key] if cnt is None else cnt
                k = ("dma", d.key)
                if k not in deps or deps[k][1] < c:
                    deps[k] = (d, c)
            else:
                k = ("eng", d.eng)
                if k not in deps or deps[k][0].idx < d.idx:
                    deps[k] = (d, None)

        for r in reads:
            w = r.w
            if w is not None:
                if not (w.eng == eng == "pe" and not w.is_dma and not is_dma):
                    add_dep(w)
        for r in writes:
            w = r.w
            if w is not None and (w.is_dma or is_dma or w.eng != eng):
                add_dep(w)
            for rd in r.rs:
                if rd.is_dma or is_dma or rd.eng != eng:
                    add_dep(rd)
        for d in extra_deps:
            add_dep(d)
        if is_dma:
            self.dma_issued[key] = self.dma_issued.get(key, 0) + 1
        op.deps = list(deps.values())
        for d, _ in op.deps:
            d.needed = True
        for r in reads:
            r.rs.append(op)
        for r in writes:
            r.w = op
            r.rs = []
        self.ops.append(op)
        if fn is not None and not is_dma:
            self.last[eng] = op
        return op

    def op(self, eng, fn, reads=(), writes=()):
        return self._add(eng, fn, reads, writes, False, None)

    def dma(self, eng, key, fn, reads=(), writes=(), inc=16):
        self.dma_inc[key] = inc
        return self._add(eng, fn, reads, writes, True, key)

    def barrier(self):
        lasts = [o for o in self.last.values()]
        dmas = {}
        for o in self.ops:
            if o.is_dma:
                dmas[o.key] = o
        for e in COMPUTE + ("sp",):
            self._add(e, None, (), (), False, None, extra_deps=lasts + list(dmas.values()))

    def emit(self):
        nc = self.nc
        cnt = {e: 0 for e in COMPUTE}
        for op in self.ops:
            if not op.is_dma and op.needed:
                cnt[op.eng] += 1
                op.ms = cnt[op.eng]
        LIM = 30000
        stack = contextlib.ExitStack()
        with stack:
            esems = {}
            for e in COMPUTE:
                n = cnt[e] // LIM + 1
                esems[e] = [stack.enter_context(nc.semaphore(f"s_{e}_{i}")) for i in range(n)]
            dsems = {}
            for k in self.dma_issued:
                dsems[k] = stack.enter_context(nc.semaphore(f"d_{k}"))
            block = stack.enter_context(nc.Block())

            def stream(ename):
                def body(eng_obj):
                    seen = {}
                    for op in self.ops:
                        if op.eng != ename:
                            continue
                        waits = {}
                        for d, c in op.deps:
                            if d.is_dma:
                                sem = dsems[d.key]; val = self.dma_inc[d.key] * c
                            else:
                                sem = esems[d.eng][(d.ms - 1) // LIM]; val = (d.ms - 1) % LIM + 1
                            if waits.get(sem, 0) < val:
                                waits[sem] = val
                        for sem, val in waits.items():
                            if seen.get(sem, 0) >= val:
                                continue
                            seen[sem] = val
                            eng_obj.wait_ge(sem, val)
                        if op.fn is None:
                            continue
                        ins = op.fn(eng_obj)
                        if op.is_dma:
                            ins.then_inc(dsems[op.key], self.dma_inc[op.key])
                        elif op.ms is not None:
                            ins.then_inc(esems[op.eng][(op.ms - 1) // LIM], 1)
                    if ename == "sp":
                        for k, n in self.dma_issued.items():
                            if seen.get(dsems[k], 0) < self.dma_inc[k] * n:
                                eng_obj.wait_ge(dsems[k], self.dma_inc[k] * n)
                return body

            block.tensor(stream("pe"))
            block.scalar(stream("act"))
            block.vector(stream("dve"))
            block.gpsimd(stream("pool"))
            block.sync(stream("sp"))


class Arena:
    def __init__(self, ap_f32, nbytes):
        self.ap = ap_f32
        self.nbytes = nbytes
        self.off = 0

    def reset(self, off=0):
        self.off = off

    def take(self, shape, dtype):
        n = 1
        for s_ in shape:
            n *= s_
        nb = n * (2 if dtype is BF16 else 4)
        off = self.off
        self.off += (nb + 63) // 64 * 64
        assert self.off <= self.nbytes, ("arena overflow", self.off, self.nbytes)
        v = self.ap[:, off // 4:(off + nb) // 4]
        if dtype is BF16:
            v = v.bitcast(BF16)
        if len(shape) == 2:
            v = v.rearrange("p (a b) -> p a b", a=shape[0])
        elif len(shape) == 3:
            v = v.rearrange("p (a b c) -> p a b c", a=shape[0], b=shape[1])
        return v


ARENA_BYTES = 124 * 1024


class Ctx:
    pass


def build(mode, stop=99):
    nc = bass.Bass("TRN2", target_bir_lowering=False)
    P = Prog(nc)
    has0 = mode in ("L0", "FUSED")
    has1 = mode in ("L1", "FUSED")

    def din(name, shape):
        return nc.dram_tensor(name, list(shape), F32, kind="ExternalInput").ap()

    c = Ctx()
    c.nc = nc; c.P = P
    xall = din("xall", [D, S])
    xown = din("xown", [D, 4 * TC])
    consts_d = din("consts", [128, 384])
    gains_d = din("gains", [128, NG])
    mask_d = {}; wo_d = {}; wgu_d = {}; wd_d = {}
    for l in ([0] if has0 else []) + ([1] if has1 else []):
        mask_d[l] = din(f"mask{l}", [128, 2 * 8 * TC])
        wo_d[l] = din(f"wo{l}", [D, D])
        wgu_d[l] = din(f"wgu{l}", [D, 2 * DFF])
        wd_d[l] = din(f"wd{l}", [DFF, D])
    if has0:
        ropeC = din("ropeC", [128, S]); ropeS = din("ropeS", [128, S])
        ropeCo = din("ropeCo", [128, 4 * TC]); ropeSo = din("ropeSo", [128, 4 * TC])
        lamv_d = din("lamv", [128, 256])
        wqkv_d = din("wqkv", [D, 3 * D])
    if has1:
        kvw_d = din("kvw", [D, 2 * D])
        wq_d = din("wq", [D, D])
    y = nc.dram_tensor("y", [D, 4 * TC], F32, kind="ExternalOutput").ap()
    if mode == "FUSED":
        X1o = [nc.dram_tensor(f"x1o{s}", [D, TC], F32).ap() for s in range(4)]
        X1g = [nc.dram_tensor(f"x1g{s}", [2 * D, TC], F32).ap() for s in range(4)]
        rX1o = [Res() for _ in range(4)]; rX1g = [Res() for _ in range(4)]

    sk = dict(kind="ExternalOutput") if DBG.get("dump") else {}
    Ksc = nc.dram_tensor("Ksc", [8, 128, S], BF16, **sk).ap()
    Vsc = nc.dram_tensor("Vsc", [S, D], BF16, **sk).ap()
    Qsc = nc.dram_tensor("Qsc", [8, 128, 4 * TC], BF16, **sk).ap()
    WGUb = nc.dram_tensor("WGUb", [2, 6, 128, 8, 512], BF16).ap()
    WDb = nc.dram_tensor("WDb", [4, 128, NJ, 256], BF16).ap()
    rWGUb = [Res() for _ in range(48)]
    rWDb = [Res() for _ in range(NJ)]
    rK = [[Res() for _ in range(NCH)] for _ in range(8)]
    rV = [Res() for _ in range(NCH)]
    rQ = [[Res() for _ in range(4)] for _ in range(8)]
    CH2RS = {}
    for r_ in range(2):
        for s_, ch_ in enumerate(OWN[r_]):
            CH2RS[ch_] = (r_, s_)

    Rt = nc.alloc_sbuf_tensor("R", [128, 8 * 4 * TC], F32)
    R3 = Rt[:, :].rearrange("p (a b) -> p a b", a=8)
    rR = [Res() for _ in range(4)]
    arena_t = nc.alloc_sbuf_tensor("arena", [128, ARENA_BYTES // 4], F32)
    A = Arena(arena_t[:, :], ARENA_BYTES)
    gains = nc.alloc_sbuf_tensor("gains_sb", [128, NG], F32)[:, :]
    rG = Res()
    cst = nc.alloc_sbuf_tensor("cst", [128, 384], BF16)[:, :]
    rC = Res()
    ones = cst[:, 0:128]; tri = cst[:, 128:256]; pm = cst[:, 256:384]
    small = nc.alloc_sbuf_tensor("small", [128, 512], F32)[:, :]
    pst = nc.alloc_psum_tensor("ps", [128, 8 * 512], F32)[:, :]
    rPS = [Res() for _ in range(8)]

    def bank(i, n=1):
        return pst[:, i * 512:(i + n) * 512]

    def mm(out, lhsT, rhs, start, stop, R, W):
        P.op("pe", lambda e: e.matmul(out, lhsT=lhsT, rhs=rhs, start=start, stop=stop), R, W)

    def act(out, in_, func, R, W, scale=1.0, bias=0.0):
        P.op("act", lambda e: e.activation(out=out, in_=in_, func=func, bias=bias, scale=scale), R, W)

    def tt(eng, out, in0, in1, op, R, W):
        P.op(eng, lambda e: e.tensor_tensor(out=out, in0=in0, in1=in1, op=op), R, W)

    def ts(eng, out, in0, s1, s2, op0, op1, R, W):
        if op1 is None:
            P.op(eng, lambda e: e.tensor_scalar(out=out, in0=in0, scalar1=s1, scalar2=None, op0=op0), R, W)
        else:
            P.op(eng, lambda e: e.tensor_scalar(out=out, in0=in0, scalar1=s1, scalar2=s2, op0=op0, op1=op1), R, W)

    def stt(eng, out, in0, scalar, in1, op0, op1, R, W):
        P.op(eng, lambda e: e.scalar_tensor_tensor(out=out, in0=in0, scalar=scalar, in1=in1, op0=op0, op1=op1), R, W)

    def cp(eng, out, in_, R, W):
        if eng == "act":
            P.op("act", lambda e: e.copy(out=out, in_=in_), R, W)
        else:
            P.op(eng, lambda e: e.tensor_copy(out=out, in_=in_), R, W)

    def dma(eng, key, out, in_, R, W):
        P.dma(eng, key, lambda e: e.dma_start(out=out, in_=in_), R, W)

    def kcview(ap2d):
        return ap2d.rearrange("(kc p) n -> p kc n", p=128)

    dma("pool", "cst", cst, consts_d, [], [rC])
    dma("sp", "gains", gains, gains_d, [], [rG])

    def norm_stats(xv, xres, sqv, sqres, psn, tln, rstd, rtmp, rrstd, nk=8, scale=1.0 / D):
        act(sqv, xv, AF.Square, [xres], [sqres])
        for kc in range(nk):
            mm(bank(psn), ones, sqv[:, kc, :], kc == 0, kc == nk - 1, [sqres, rC], [rPS[psn]])
        act(tln, bank(psn), AF.Ln, [rPS[psn]], [rtmp], scale=scale, bias=EPS)
        act(rstd, tln, AF.Exp, [rtmp], [rrstd], scale=-0.5)

    def phase_proj(layer):
        A.reset()
        L0 = layer == 0
        if L0:
            WA = A.take([8, 3 * D], BF16)
            wsrc = kcview(wqkv_d)
            qoff, koff, voff = 0, D, 2 * D
        else:
            WA = A.take([8, 3 * D], BF16)
            qoff, koff, voff = 0, D, 2 * D
        rWK, rWV, rWQ = Res(), Res(), Res()
        if L0:
            dma("pool", "wk", WA[:, :, koff:koff + D], wsrc[:, :, D:2 * D], [], [rWK])
            dma("pool", "wv", WA[:, :, voff:voff + D], wsrc[:, :, 2 * D:3 * D], [], [rWV])
            dma("pool", "wq", WA[:, :, qoff:qoff + D], wsrc[:, :, 0:D], [], [rWQ])
        else:
            kvs = kcview(kvw_d)
            dma("pool", "wk", WA[:, :, koff:koff + D], kvs[:, :, 0:D], [], [rWK])
            dma("pool", "wv", WA[:, :, voff:voff + D], kvs[:, :, D:2 * D], [], [rWV])
            dma("pool", "wq", WA[:, :, qoff:qoff + D], kcview(wq_d), [], [rWQ])
        XC = A.take([8, TC], F32); rXC = Res()
        HT = A.take([8, TC], BF16); rHT = Res()
        SQ = A.take([8, TC], BF16); rSQ = Res()
        TLN = A.take([1, TC], F32)[:, 0, :]; rTLN = Res()
        RSTD = A.take([1, TC], F32)[:, 0, :]; rRSTD = Res()
        RCb = [A.take([1, TC], F32)[:, 0, :] for _ in range(2)]; rRC = [Res(), Res()]
        RSb = [A.take([1, TC], F32)[:, 0, :] for _ in range(2)]; rRS = [Res(), Res()]
        KB = [A.take([1, TC], BF16)[:, 0, :] for _ in range(2)]; rKB = [Res(), Res()]
        T1 = [A.take([1, TC], F32)[:, 0, :] for _ in range(2)]; rT1 = [Res(), Res()]
        T2 = [A.take([1, TC], F32)[:, 0, :] for _ in range(2)]; rT2 = [Res(), Res()]
        KR = [A.take([1, TC], BF16)[:, 0, :] for _ in range(3)]; rKR = [Res(), Res(), Res()]
        VS = [A.take([1, D], BF16)[:, 0, :] for _ in range(2)]; rVS = [Res(), Res()]
        PSN, PSK, PSP, PSV = 0, (1, 2), (3, 4), (5, 6)
        cnt = {"k": 0, "v": 0}
        def norm_chunk(xv, xres, gcol):
            norm_stats(xv, xres, SQ, rSQ, PSN, TLN, RSTD, rTLN, rRSTD)
            for kc in range(8):
                stt("dve", HT[:, kc, :], xv[:, kc, :], gains[:, gcol + kc:gcol + kc + 1], RSTD,
                    ALU.mult, ALU.mult, [xres, rG, rRSTD], [rHT])

        def proj_fm(wres, woff, hh, dst_dram, dst_res, rc, rs_, rcres, rsres):
            i = cnt["k"]; cnt["k"] += 1
            b = i % 2
            for kc in range(8):
                mm(bank(PSK[b]), WA[:, kc, woff + hh * 128:woff + (hh + 1) * 128], HT[:, kc, :],
                   kc == 0, kc == 7, [wres, rHT], [rPS[PSK[b]]])
            kr = i % 3
            if L0:
                cp("act", KB[b], bank(PSK[b]), [rPS[PSK[b]]], [rKB[b]])

            def epilogue():
                if L0:
                    mm(bank(PSP[b]), pm, KB[b], True, True, [rC, rKB[b]], [rPS[PSP[b]]])
                    tt("dve", T1[b], bank(PSK[b]), rc, ALU.mult, [rPS[PSK[b]], rcres], [rT1[b], rPS[PSK[b]]])
                    tt("dve", T2[b], bank(PSP[b]), rs_, ALU.mult, [rPS[PSP[b]], rsres], [rT2[b]])
                    tt("pool", KR[kr], T1[b], T2[b], ALU.add, [rT1[b], rT2[b]], [rKR[kr]])
                else:
                    cp("act", KR[kr], bank(PSK[b]), [rPS[PSK[b]]], [rKR[kr]])
                dma("sp", f"kst{kr}", dst_dram, KR[kr], [rKR[kr]], [dst_res])
            return epilogue

        def proj_heads(kind, i, rb, heads):
            prev = None
            for hh in heads:
                if kind == "kv":
                    ep = proj_fm(rWK, koff, hh, Ksc[hh, :, i * TC:(i + 1) * TC], rK[hh][i],
                                 RCb[rb], RSb[rb], rRC[rb], rRS[rb])
                else:
                    ep = proj_fm(rWQ, qoff, hh, Qsc[hh, :, i * TC:(i + 1) * TC], rQ[hh][i],
                                 RCb[rb], RSb[rb], rRC[rb], rRS[rb])
                if prev is not None:
                    prev()
                prev = ep
            prev()

        xall_v = kcview(xall)
        xown_v = kcview(xown)
        items = [("kv", ci) for ci in range(NCH)] + [("q", s) for s in range(4)]
        fused_l1 = (mode == "FUSED" and not L0)

        def xview(t):
            kind, i = items[t]
            if kind == "kv":
                return XC, rXC, (G_MIXPRE if L0 else G_KV)
            return R3[:, :, i * TC:(i + 1) * TC], rR[i], G_MIXPRE + 8 * layer

        def prep(t):
            kind, i = items[t]
            rb = t % 2
            if kind == "kv":
                if fused_l1:
                    r_, s_ = CH2RS[i]
                    src = X1g[s_].rearrange("(r kc p) t -> r p kc t", r=2, p=128)[r_]
                    dma("sp", "xc", XC, src, [rX1g[s_]], [rXC])
                else:
                    dma("sp", "xc", XC, xall_v[:, :, i * TC:(i + 1) * TC], [], [rXC])
                if L0:
                    dma("sp", f"rc{rb}", RCb[rb], ropeC[:, i * TC:(i + 1) * TC], [], [rRC[rb]])
                    dma("sp", f"rs{rb}", RSb[rb], ropeS[:, i * TC:(i + 1) * TC], [], [rRS[rb]])
            else:
                if not fused_l1:
                    dma("sp", "xr", R3[:, :, i * TC:(i + 1) * TC], xown_v[:, :, i * TC:(i + 1) * TC], [], [rR[i]])
                if L0:
                    dma("sp", f"rc{rb}", RCb[rb], ropeCo[:, i * TC:(i + 1) * TC], [], [rRC[rb]])
                    dma("sp", f"rs{rb}", RSb[rb], ropeSo[:, i * TC:(i + 1) * TC], [], [rRS[rb]])

        def stats(t):
            xv, xres, gcol = xview(t)
            norm_stats(xv, xres, SQ, rSQ, PSN, TLN, RSTD, rTLN, rRSTD)

        def scale(t):
            xv, xres, gcol = xview(t)
            for kc in range(8):
                stt("dve", HT[:, kc, :], xv[:, kc, :], gains[:, gcol + kc:gcol + kc + 1], RSTD,
                    ALU.mult, ALU.mult, [xres, rG, rRSTD], [rHT])

        def part1(t):
            kind, i = items[t]
            proj_heads(kind, i, t % 2, range(8 if kind == "kv" else 4))

        def part2(t):
            kind, i = items[t]
            rb = t % 2
            if kind == "q":
                proj_heads(kind, i, rb, range(4, 8))
                return
            ci = i
            for tb in range(4):
                vb = cnt["v"] % 2; cnt["v"] += 1
                for half in range(2):
                    pb = PSV[half]
                    for kc in range(8):
                        mm(bank(pb), HT[:, kc, tb * 128:(tb + 1) * 128],
                           WA[:, kc, voff + half * 512:voff + (half + 1) * 512],
                           kc == 0, kc == 7, [rHT, rWV], [rPS[pb]])
                    cp("act" if half == 0 else "dve", VS[vb][:, half * 512:(half + 1) * 512], bank(pb),
                       [rPS[pb]], [rVS[vb]])
                dma("sp", f"vst{vb}", Vsc[ci * TC + tb * 128:ci * TC + (tb + 1) * 128, :], VS[vb],
                    [rVS[vb]], [rV[ci]])

        prep(0); stats(0); scale(0)
        for t in range(len(items)):
            if t + 1 < len(items):
                prep(t + 1)
            part1(t)
            if t + 1 < len(items):
                stats(t + 1)
            part2(t)
            if t + 1 < len(items):
                scale(t + 1)

    AO_BYTES = 8 * 4 * TC * 2

    def make_caster(layer):
        STG = [A.take([1, 1024], BF16)[:, 0, :] for _ in range(2)]; rSTG = [Res(), Res()]
        wgu_src = wgu_d[layer]; wd_src = wd_d[layer]
        jobs = [("gu", kc, gu, gp) for kc in range(8) for gu in range(2) for gp in range(3)]
        jobs += [("d", j, 0, 0) for j in range(NJ)]
        pend = []
        st = {"n": 0}

        def issue(job, sb_):
            kind, a_, gu, gp = job
            if kind == "gu":
                w = 1024 if gp < 2 else 768
                c0 = gu * DFF + gp * 1024
                dma("pool", f"stg{sb_}", STG[sb_][:, 0:w], wgu_src[a_ * 128:(a_ + 1) * 128, c0:c0 + w],
                    [], [rSTG[sb_]])
            else:
                dma("pool", f"stg{sb_}", STG[sb_], wd_src[a_ * 128:(a_ + 1) * 128, :], [], [rSTG[sb_]])

        def store(job, sb_):
            kind, a_, gu, gp = job
            if kind == "gu":
                kc = a_
                rr = rWGUb[(kc * 2 + gu) * 3 + gp]
                if gp < 2:
                    dma("sp", f"stgs{sb_}", WGUb[gu, 2 * gp:2 * gp + 2, :, kc, :].rearrange("g p c -> p g c"),
                        STG[sb_].rearrange("p (g c) -> p g c", g=2), [rSTG[sb_]], [rr])
                else:
                    dma("sp", f"stgs{sb_}", WGUb[gu, 4, :, kc, :], STG[sb_][:, 0:512], [rSTG[sb_]], [rr])
                    dma("sp", f"stgs{sb_}", WGUb[gu, 5, :, kc, 0:256], STG[sb_][:, 512:768], [rSTG[sb_]], [rr])
            else:
                j = a_
                dma("sp", f"stgs{sb_}", WDb[:, :, j, :].rearrange("q p c -> p q c"),
                    STG[sb_].rearrange("p (q c) -> p q c", q=4), [rSTG[sb_]], [rWDb[j]])

        def step(flush=False):
            if pend:
                pj, psb = pend.pop(0)
                store(pj, psb)
            if jobs and not flush:
                sb_ = st["n"] % 2; st["n"] += 1
                job = jobs.pop(0)
                issue(job, sb_)
                pend.append((job, sb_))
            return bool(jobs) or bool(pend)
        return step

    def load_masks(layer):
        MASK = A.take([2, 8, TC], BF16); rM = Res()
        dma("pool", "mask", MASK.rearrange("p a b c -> p (a b c)"), mask_d[layer], [], [rM])
        return MASK, rM

    def load_head(hh, KH, VH, QH, rKH, rVH, rQH):
        b = hh % 2
        dma("sp", f"kh{b}", KH[b], Ksc[hh], [rK[hh][ci] for ci in range(NCH)], [rKH[b]])
        dma("sp", f"vh{b}", VH[b], Vsc[:, hh * 128:(hh + 1) * 128].rearrange("(kb p) d -> p kb d", p=128),
            [rV[ci] for ci in range(NCH)], [rVH[b]])
        dma("sp", f"qh{b}", QH[b], Qsc[hh], [rQ[hh][s] for s in range(4)], [rQH[b]])

    def phase_attn0():
        A.reset()
        AO = A.take([8, 4 * TC], BF16)
        c.AO = AO; c.rAO = [[Res() for _ in range(4)] for _ in range(8)]
        MASK, rM = load_masks(0)
        KH = [A.take([1, S], BF16)[:, 0, :] for _ in range(2)]; rKH = [Res(), Res()]
        VH = [A.take([32, 128], BF16) for _ in range(2)]; rVH = [Res(), Res()]
        QH = [A.take([1, 4 * TC], BF16)[:, 0, :] for _ in range(2)]; rQH = [Res(), Res()]
        NE = 3
        E = [A.take([2, TC], BF16) for _ in range(NE)]; rEc = [[Res(), Res()] for _ in range(NE)]
        f32t = lambda: A.take([1, TC], F32)[:, 0, :]
        R1, R2, T1, T2, OO, TLN, RSTD = [f32t() for _ in range(7)]
        rR1, rR2, rT1, rT2, rOO, rTLN, rRSTD = [Res() for _ in range(7)]
        SQo = A.take([1, TC], BF16); rSQo = Res()
        caster = make_caster(0)
        lamv = small[:, 0:256]; rL = Res()
        dma("sp", "lamv", lamv, lamv_d, [], [rL])
        pr = small[:, 256:384]; sm = small[:, 384:400]; rSM = Res()
        tt("dve", pr[:, 0:64], lamv[:, 0:64], lamv[:, 64:128], ALU.mult, [rL], [rSM])
        tt("dve", pr[:, 64:128], lamv[:, 128:192], lamv[:, 192:256], ALU.mult, [rL], [rSM])
        P.op("dve", lambda e: e.tensor_reduce(out=sm[:, 0:1], in_=pr[:, 0:64], axis=AX.X, op=ALU.add), [rSM], [rSM])
        P.op("dve", lambda e: e.tensor_reduce(out=sm[:, 1:2], in_=pr[:, 64:128], axis=AX.X, op=ALU.add), [rSM], [rSM])
        act(sm[:, 2:4], sm[:, 0:2], AF.Exp, [rSM], [rSM])
        tt("dve", sm[:, 4:5], sm[:, 3:4], sm[:, 2:3], ALU.subtract, [rSM], [rSM])
        ts("dve", sm[:, 5:6], sm[:, 4:5], -LAMBDA_INIT, None, ALU.add, None, [rSM], [rSM])
        ts("dve", sm[:, 6:7], gains[:, G_SUB:G_SUB + 1], 1.0 - LAMBDA_INIT, None, ALU.mult, None, [rSM, rG], [rSM])
        neglam = sm[:, 5:6]; gsub = sm[:, 6:7]
        O1, O2, L1, L2 = 4, 5, 6, 7
        steps = [(hh, s, kb) for hh in range(8) for s in range(4) for kb in range(8 * (s + 1))]
        n = len(steps)

        def emit_S(i):
            hh, s, kb = steps[i]
            hb = hh % 2; sb = i % 2
            qs = slice(s * TC, (s + 1) * TC); ks = slice(kb * 128, (kb + 1) * 128)
            mm(bank(2 * sb), KH[hb][0:64, ks], QH[hb][0:64, qs], True, True,
               [rKH[hb], rQH[hb]], [rPS[2 * sb]])
            mm(bank(2 * sb + 1), KH[hb][64:128, ks], QH[hb][64:128, qs], True, True,
               [rKH[hb], rQH[hb]], [rPS[2 * sb + 1]])

        def emit_rest(i):
            hh, s, kb = steps[i]
            hb = hh % 2; sb = i % 2; eb = i % NE
            nkb = 8 * (s + 1)
            qs = slice(s * TC, (s + 1) * TC)
            if s == 0 and kb == 0 and hh + 1 < 8:
                load_head(hh + 1, KH, VH, QH, rKH, rVH, rQH)
            for cc in range(2):
                act(E[eb][:, cc, :], bank(2 * sb + cc), AF.Exp, [rPS[2 * sb + cc]], [rEc[eb][cc]], scale=0.125)
                if kb >= nkb - 8:
                    mk = MASK[:, s % 2, kb - (nkb - 8), :]
                    tt("dve", E[eb][:, cc, :], E[eb][:, cc, :], mk, ALU.mult, [rEc[eb][cc], rM], [rEc[eb][cc]])
            st, sp_ = kb == 0, kb == nkb - 1
            mm(bank(O1), VH[hb][:, kb, :], E[eb][:, 0, :], st, sp_, [rVH[hb], rEc[eb][0]], [rPS[O1]])
            mm(bank(L1), ones, E[eb][:, 0, :], st, sp_, [rC, rEc[eb][0]], [rPS[L1]])
            mm(bank(O2), VH[hb][:, kb, :], E[eb][:, 1, :], st, sp_, [rVH[hb], rEc[eb][1]], [rPS[O2]])
            mm(bank(L2), ones, E[eb][:, 1, :], st, sp_, [rC, rEc[eb][1]], [rPS[L2]])
            if kb == nkb - 1:
                P.op("dve", lambda e: e.reciprocal(out=R1, in_=bank(L1)), [rPS[L1]], [rR1])
                P.op("dve", lambda e: e.reciprocal(out=R2, in_=bank(L2)), [rPS[L2]], [rR2])
                tt("dve", T1, bank(O1), R1, ALU.mult, [rPS[O1], rR1], [rT1])
                tt("dve", T2, bank(O2), R2, ALU.mult, [rPS[O2], rR2], [rT2])
                stt("dve", OO, T2, neglam, T1, ALU.mult, ALU.add, [rT1, rT2, rSM], [rOO])
                act(SQo[:, 0, :], OO, AF.Square, [rOO], [rSQo])
                pn = 2 * (i % 2)
                mm(bank(pn), ones, SQo[:, 0, :], True, True, [rC, rSQo], [rPS[pn]])
                act(TLN, bank(pn), AF.Ln, [rPS[pn]], [rTLN], scale=1.0 / 128, bias=EPS)
                act(RSTD, TLN, AF.Exp, [rTLN], [rRSTD], scale=-0.5)
                stt("dve", AO[:, hh, qs], OO, gsub, RSTD, ALU.mult, ALU.mult, [rOO, rSM, rRSTD], [c.rAO[hh][s]])

        load_head(0, KH, VH, QH, rKH, rVH, rQH)
        emit_S(0)
        for i in range(n):
            if i + 1 < n:
                emit_S(i + 1)
            emit_rest(i)
            if i % 8 == 4:
                caster()
        while caster():
            pass

    def phase_attn1():
        A.reset()
        AO = A.take([8, 4 * TC], BF16)
        c.AO = AO; c.rAO = [[Res() for _ in range(4)] for _ in range(8)]
        MASK, rM = load_masks(1)
        KH = [A.take([1, S], BF16)[:, 0, :] for _ in range(2)]; rKH = [Res(), Res()]
        VH = [A.take([32, 128], BF16) for _ in range(2)]; rVH = [Res(), Res()]
        QH = [A.take([1, 4 * TC], BF16)[:, 0, :] for _ in range(2)]; rQH = [Res(), Res()]
        EE = [A.take([2, TC], F32) for _ in range(3)]; rEE = [Res() for _ in range(3)]
        SP = [A.take([2, TC], BF16) for _ in range(3)]; rSP = [Res() for _ in range(3)]
        CAR = [A.take([2, TC], BF16) for _ in range(3)]; rCAR = [Res() for _ in range(3)]
        E2 = [A.take([2, TC], BF16) for _ in range(2)]; rE2 = [Res() for _ in range(2)]
        AA = [A.take([2, TC], BF16) for _ in range(2)]; rAA = [Res() for _ in range(2)]
        caster = make_caster(1)
        CB, OB = 4, 6
        fl = lambda v: v.rearrange("p a b -> p (a b)")
        steps = [(hp, s, it, 8 * (s + 1) - 1 - it) for hp in range(8) for s in range(4) for it in range(8 * (s + 1))]
        n = len(steps)

        def info(j):
            hp, s, it, kb = steps[j]
            return hp, s, it, kb, hp % 2, 8 * (s + 1), slice(s * TC, (s + 1) * TC)

        def op_Z(j):
            hp, s, it, kb, hb, nkb, qs = info(j)
            sb = j % 2
            ks = slice(kb * 128, (kb + 1) * 128)
            for u in range(2):
                us = slice(u * 64, (u + 1) * 64)
                mm(bank(2 * sb + u), KH[hb][us, ks], QH[hb][us, qs], True, True,
                   [rKH[hb], rQH[hb]], [rPS[2 * sb + u]])

        def op_expZ(j):
            sb = j % 2; e3 = j % 3
            act(fl(EE[e3]), bank(2 * sb, 2), AF.Exp, [rPS[2 * sb], rPS[2 * sb + 1]], [rEE[e3]], scale=0.125)

        def op_mask(j):
            hp, s, it, kb, hb, nkb, qs = info(j)
            e3 = j % 3
            if kb >= nkb - 8:
                mk = MASK[:, s % 2, kb - (nkb - 8), :]
                for u in range(2):
                    tt("dve", EE[e3][:, u, :], EE[e3][:, u, :], mk, ALU.mult, [rEE[e3], rM], [rEE[e3]])

        def op_ln(j):
            e3 = j % 3
            act(fl(SP[e3]), fl(EE[e3]), AF.Ln, [rEE[e3]], [rSP[e3]], bias=1.0)

        def op_carry(j):
            hp, s, it, kb, hb, nkb, qs = info(j)
            e3 = j % 3
            if it >= 1 and it < nkb - 1:
                if it == 1:
                    tt("dve", fl(CAR[(j + 1) % 3]), fl(SP[(j - 1) % 3]), fl(SP[e3]), ALU.add,
                       [rSP[(j - 1) % 3], rSP[e3]], [rCAR[(j + 1) % 3]])
                else:
                    tt("dve", fl(CAR[(j + 1) % 3]), fl(CAR[j % 3]), fl(SP[e3]), ALU.add,
                       [rCAR[j % 3], rSP[e3]], [rCAR[(j + 1) % 3]])

        def op_C(j):
            hp, s, it, kb, hb, nkb, qs = info(j)
            e3 = j % 3
            for u in range(2):
                if it == 1:
                    mm(bank(CB + u), ones, SP[(j - 1) % 3][:, u, :], True, False, [rC, rSP[(j - 1) % 3]], [rPS[CB + u]])
                elif it >= 2:
                    mm(bank(CB + u), ones, CAR[j % 3][:, u, :], True, False, [rC, rCAR[j % 3]], [rPS[CB + u]])
            for u in range(2):
                mm(bank(CB + u), tri, SP[e3][:, u, :], it == 0, True, [rC, rSP[e3]], [rPS[CB + u]])

        def op_expC(j):
            act(fl(E2[j % 2]), bank(CB, 2), AF.Exp, [rPS[CB], rPS[CB + 1]], [rE2[j % 2]], scale=-1.0)

        def op_a_av(j):
            hp, s, it, kb, hb, nkb, qs = info(j)
            e3 = j % 3
            if s == 0 and it == 0 and hp + 1 < 8:
                load_head(hp + 1, KH, VH, QH, rKH, rVH, rQH)
            tt("dve", fl(AA[j % 2]), fl(EE[e3]), fl(E2[j % 2]), ALU.mult, [rEE[e3], rE2[j % 2]], [rAA[j % 2]])
            for u in range(2):
                mm(bank(OB + u)[0:64, :], VH[hb][:, kb, u * 64:(u + 1) * 64], AA[j % 2][:, u, :],
                   it == 0, it == nkb - 1, [rVH[hb], rAA[j % 2]], [rPS[OB + u]])
            if it == nkb - 1:
                for u in range(2):
                    cp("act" if u == 0 else "dve", AO[u * 64:(u + 1) * 64, hp, qs],
                       bank(OB + u)[0:64, :], [rPS[OB + u]], [c.rAO[hp][s]])

        load_head(0, KH, VH, QH, rKH, rVH, rQH)
        op_Z(0); op_expZ(0); op_mask(0); op_ln(0); op_Z(1)
        for i in range(n + 1):
            if i < n:
                op_carry(i)
                op_C(i)
            if i + 1 < n:
                op_expZ(i + 1)
            if i + 2 < n:
                op_Z(i + 2)
            if i >= 1:
                op_a_av(i - 1)
            if i < n:
                op_expC(i)
            if i + 1 < n:
                op_mask(i + 1)
                op_ln(i + 1)
            if i % 8 == 4:
                caster()
        while caster():
            pass

    def post_norm_add(s, MT, rMT, gcol, SQ, rSQ, psn, TLN, RSTD, rTLN, rRSTD, TMP, rTMP):
        norm_stats(MT, rMT, SQ, rSQ, psn, TLN, RSTD, rTLN, rRSTD)
        for mc in range(8):
            b = mc % 2
            stt("dve", TMP[b], MT[:, mc, :], gains[:, gcol + mc:gcol + mc + 1], RSTD, ALU.mult, ALU.mult,
                [rMT, rG, rRSTD], [rTMP[b]])
            tt("pool", R3[:, mc, s * TC:(s + 1) * TC], R3[:, mc, s * TC:(s + 1) * TC], TMP[b], ALU.add,
               [rR[s], rTMP[b]], [rR[s]])

    def phase_wo(layer):
        A.reset(AO_BYTES)
        WB = A.take([8, D], BF16); rWB = Res()
        dma("pool", "wo", WB, kcview(wo_d[layer]), [], [rWB])
        MT = A.take([8, TC], F32); rMT = Res()
        SQ = A.take([8, TC], BF16); rSQ = Res()
        TLN = A.take([1, TC], F32)[:, 0, :]; rTLN = Res()
        RSTD = A.take([1, TC], F32)[:, 0, :]; rRSTD = Res()
        TMP = [A.take([1, TC], F32)[:, 0, :] for _ in range(2)]; rTMP = [Res(), Res()]
        PSM, PSN = (0, 1), 2
        for s in range(4):
            qs = slice(s * TC, (s + 1) * TC)
            for mc in range(8):
                b = PSM[mc % 2]
                for hc in range(8):
                    mm(bank(b), WB[:, hc, mc * 128:(mc + 1) * 128], c.AO[:, hc, qs], hc == 0, hc == 7,
                       [rWB, c.rAO[hc][s]], [rPS[b]])
                cp("act" if mc % 2 == 0 else "dve", MT[:, mc, :], bank(b), [rPS[b]], [rMT])
            post_norm_add(s, MT, rMT, G_MIXPOST + 8 * layer, SQ, rSQ, PSN, TLN, RSTD, rTLN, rRSTD, TMP, rTMP)

    def phase_ffn(layer, final):
        wgu_ap, wd_ap = wgu_d[layer], wd_d[layer]
        A.reset()
        HT = A.take([8, TC], BF16); rHT = Res()
        SQ = A.take([8, TC], BF16); rSQ = Res()
        TLN = A.take([1, TC], F32)[:, 0, :]; rTLN = Res()
        RSTD = A.take([1, TC], F32)[:, 0, :]; rRSTD = Res()
        WG = [A.take([8, 512], BF16) for _ in range(2)]; rWG = [Res(), Res()]
        WU = [A.take([8, 512], BF16) for _ in range(2)]; rWU = [Res(), Res()]
        WD = [A.take([NJ, 256], BF16) for _ in range(2)]; rWD = [Res(), Res()]
        ACTB = A.take([NJ, TC], BF16); rACT = [Res() for _ in range(NJ)]
        SG = [A.take([1, TC], F32)[:, 0, :] for _ in range(2)]; rSG = [Res(), Res()]
        FT = A.take([8, TC], F32); rFT = Res()
        TMP = [A.take([1, TC], F32)[:, 0, :] for _ in range(2)]; rTMP = [Res(), Res()]
        wgu_v = kcview(wgu_ap)
        wd_v = wd_ap.rearrange("(j p) n -> p j n", p=128)
        PG, PU, PD, PSN = (0, 1), (2, 3), (4, 5), 6
        groups = [(0, 4), (4, 4), (8, 4), (12, 4), (16, 4), (20, 2)]
        gi = 0; qi = 0
        for s in range(4):
            qs = slice(s * TC, (s + 1) * TC)
            xv = R3[:, :, qs]
            norm_stats(xv, rR[s], SQ, rSQ, PSN, TLN, RSTD, rTLN, rRSTD)
            gcol = G_FFNPRE + 8 * layer
            for kc in range(8):
                stt("dve", HT[:, kc, :], xv[:, kc, :], gains[:, gcol + kc:gcol + kc + 1], RSTD,
                    ALU.mult, ALU.mult, [rR[s], rG, rRSTD], [rHT])
            for (j0, ng) in groups:
                wb = gi % 2; gi += 1
                g_ = j0 // 4
                dma("sp", f"wg{wb}", WG[wb][:, :, 0:ng * 128], WGUb[0, g_, :, :, 0:ng * 128], rWGUb, [rWG[wb]])
                dma("sp", f"wu{wb}", WU[wb][:, :, 0:ng * 128], WGUb[1, g_, :, :, 0:ng * 128], rWGUb, [rWU[wb]])
                for jj in range(ng):
                    j = j0 + jj; b = j % 2
                    for kc in range(8):
                        mm(bank(PG[b]), WG[wb][:, kc, jj * 128:(jj + 1) * 128], HT[:, kc, :], kc == 0, kc == 7,
                           [rWG[wb], rHT], [rPS[PG[b]]])
                    for kc in range(8):
                        mm(bank(PU[b]), WU[wb][:, kc, jj * 128:(jj + 1) * 128], HT[:, kc, :], kc == 0, kc == 7,
                           [rWU[wb], rHT], [rPS[PU[b]]])
                    act(SG[b], bank(PG[b]), AF.Silu, [rPS[PG[b]]], [rSG[b]])
                    tt("dve", ACTB[:, j, :], SG[b], bank(PU[b]), ALU.mult, [rSG[b], rPS[PU[b]]], [rACT[j]])
            for q in range(4):
                wb = qi % 2; qi += 1
                dma("sp", f"wd{wb}", WD[wb], WDb[q], rWDb, [rWD[wb]])
                for m2 in range(2):
                    mc = q * 2 + m2; b = PD[mc % 2]
                    for j in range(NJ):
                        mm(bank(b), WD[wb][:, j, m2 * 128:(m2 + 1) * 128], ACTB[:, j, :], j == 0, j == NJ - 1,
                           [rWD[wb], rACT[j]], [rPS[b]])
                    cp("dve" if mc % 2 == 0 else "act", FT[:, mc, :], bank(b), [rPS[b]], [rFT])
            post_norm_add(s, FT, rFT, G_FFNPOST + 8 * layer, SQ, rSQ, PSN, TLN, RSTD, rTLN, rRSTD, TMP, rTMP)
            if final:
                dma("pool", "yst", kcview(y)[:, :, qs], R3[:, :, qs], [rR[s]], [])
            else:
                dma("pool", "x1st", kcview(X1o[s]), R3[:, :, qs], [rR[s]], [rX1o[s]])
                P.dma("pool", "cc", (lambda s=s: lambda e: e.collective_compute(
                    "AllGather", ALU.bypass, replica_groups=[[0, 1], [2, 3], [4, 5], [6, 7]],
                    ins=[X1o[s]], outs=[X1g[s]]))(), [rX1o[s]], [rX1g[s]], inc=1)

    layers = [0] if mode == "L0" else [1] if mode == "L1" else [0, 1]
    for li, layer in enumerate(layers):
        last = li == len(layers) - 1
        phase_proj(layer); P.barrier()
        (phase_attn0 if layer == 0 else phase_attn1)(); P.barrier()
        phase_wo(layer); P.barrier()
        phase_ffn(layer, last); P.barrier()
    P.emit()
    return nc


def _consts():
    k = np.arange(128)
    ones = np.ones((128, 128), np.float32)
    tri = (k[:, None] >= k[None, :]).astype(np.float32)
    pm = np.zeros((128, 128), np.float32)
    for m in range(128):
        d = m % 64
        if d < 8:
            pm[m + 8, m] = -1.0
        elif d < 16:
            pm[m - 8, m] = 1.0
    return np.concatenate([ones, tri, pm], axis=1)


def _rope_tables():
    pos = np.arange(S, dtype=np.float32)
    inv = (np.float32(500000.0) ** (-np.arange(0, 16, 2, dtype=np.float32) / np.float32(16))).astype(np.float32)
    ang = (pos[:, None] * inv[None, :]).astype(np.float32)
    cos = np.cos(ang).astype(np.float32); sin = np.sin(ang).astype(np.float32)
    C = np.ones((128, S), np.float32); Sn = np.zeros((128, S), np.float32)
    for p in range(128):
        d = p % 64
        if d < 16:
            C[p] = cos[:, d % 8]; Sn[p] = sin[:, d % 8]
    return C, Sn


def _masks(r, strict):
    k = np.arange(128)[:, None]; q = np.arange(TC)[None, :]
    out = np.zeros((128, 2, 8, TC), np.float32)
    for par in range(2):
        own = OWN[r][par]
        hi = (own % 2 == 1)
        for p in range(8):
            lhs = 128 * p + k - (512 if hi else 0)
            out[:, par, p, :] = (lhs < q) if strict else (lhs <= q)
    return out.reshape(128, 2 * 8 * TC)


def _pack_gains(inp):
    g = np.zeros((128, NG), np.float32)

    def put(col, vec):
        g[:, col:col + 8] = np.asarray(vec, np.float32).reshape(8, 128).T

    for l in range(2):
        put(G_MIXPRE + 8 * l, inp["mix_pre_g"][l]); put(G_MIXPOST + 8 * l, inp["mix_post_g"][l])
        put(G_FFNPRE + 8 * l, inp["ffn_pre_g"][l]); put(G_FFNPOST + 8 * l, inp["ffn_post_g"][l])
    put(G_KV, inp["kv_norm_g"])
    g[:, G_SUB] = np.asarray(inp["a_subln_g"], np.float32).reshape(128)
    return g


_NC_CACHE = {}


def _get_nc(mode):
    if mode not in _NC_CACHE:
        _NC_CACHE[mode] = build(mode)
    return _NC_CACHE[mode]


def _own_cols(xT, r):
    return np.ascontiguousarray(np.concatenate([xT[:, ch * TC:(ch + 1) * TC] for ch in OWN[r]], axis=1))


def _scatter(ys):
    out = np.zeros((4, S, D), np.float32)
    for cidx in range(8):
        b, r = cidx // 2, cidx % 2
        yT = ys[cidx]
        for s_, ch in enumerate(OWN[r]):
            out[b, ch * TC:(ch + 1) * TC, :] = yT[:, s_ * TC:(s_ + 1) * TC].T
    return out


def make_maps(inp, mode, x_in=None):
    f = lambda a: np.ascontiguousarray(np.asarray(a, np.float32))
    consts = _consts()
    gains = _pack_gains(inp)
    x = f(inp["x"]) if x_in is None else x_in
    has0 = mode in ("L0", "FUSED"); has1 = mode in ("L1", "FUSED")
    if has0:
        C, Sn = _rope_tables()
        lamv = np.ascontiguousarray(np.concatenate(
            [np.broadcast_to(np.asarray(inp[k], np.float32).reshape(1, 64), (128, 64))
             for k in ("a_lambda_q1", "a_lambda_k1", "a_lambda_q2", "a_lambda_k2")], axis=1))
    maps = []
    for cidx in range(8):
        b, r = cidx // 2, cidx % 2
        xT = np.ascontiguousarray(x[b].T)
        m = {"xall": xT, "xown": _own_cols(xT, r), "consts": consts, "gains": gains}
        if has0:
            m.update({"mask0": _masks(r, False), "wo0": f(inp["a_w_o"][0]), "wgu0": f(inp["ffn_w_gate_up"][0]),
                      "wd0": f(inp["ffn_w_down"][0]), "ropeC": C, "ropeS": Sn, "ropeCo": _own_cols(C, r),
                      "ropeSo": _own_cols(Sn, r), "lamv": lamv, "wqkv": f(inp["a_w_qkv"][0])})
        if has1:
            m.update({"mask1": _masks(r, True), "wo1": f(inp["b_w_o"][0]), "wgu1": f(inp["ffn_w_gate_up"][1]),
                      "wd1": f(inp["ffn_w_down"][1]), "kvw": f(inp["kv_w"]), "wq": f(inp["b_w_q"][0])})
        maps.append(m)
    return maps


def kernel_unfused(**inp):
    inp = {k: np.asarray(v) for k, v in inp.items()}
    res = run_bass_kernel_spmd(_get_nc("L0"), make_maps(inp, "L0"), core_ids=list(range(8)))
    x1 = _scatter([res.results[i]["y"] for i in range(8)])
    res = run_bass_kernel_spmd(_get_nc("L1"), make_maps(inp, "L1", x1), core_ids=list(range(8)))
    return _scatter([res.results[i]["y"] for i in range(8)])


def kernel(**inp):
    inp = {k: np.asarray(v) for k, v in inp.items()}
    res = run_bass_kernel_spmd(_get_nc("FUSED"), make_maps(inp, "FUSED"), core_ids=list(range(8)))
    return _scatter([res.results[i]["y"] for i in range(8)])
```

```python
import contextlib
import numpy as np
import concourse.bass as bass
import concourse.mybir as mybir
from concourse.bass_utils import run_bass_kernel_spmd

F32 = mybir.dt.float32
BF16 = mybir.dt.bfloat16
AF = mybir.ActivationFunctionType
ALU = mybir.AluOpType
AX = mybir.AxisListType

S = 4096
D = 1024
TC = 512
NCH = 8
DFF = 2816
NJ = 22
OWN = ([0, 3, 4, 7], [1, 2, 5, 6])
EPS = 1e-6
LAMBDA_INIT = 0.8 - 0.6
G_MIXPRE, G_MIXPOST, G_FFNPRE, G_FFNPOST, G_KV, G_SUB = 0, 16, 32, 48, 64, 72
NG = 80

COMPUTE = ("pe", "act", "dve", "pool")
DBG = {}


class Res:
    __slots__ = ("name", "w", "rs")

    def __init__(self, name=""):
        self.name = name
        self.w = None
        self.rs = []


class Op:
    __slots__ = ("eng", "fn", "deps", "is_dma", "key", "needed", "ms", "idx")


class Prog:
    def __init__(self, nc):
        self.nc = nc
        self.ops = []
        self.dma_issued = {}
        self.dma_inc = {}
        self.last = {}

    def _add(self, eng, fn, reads, writes, is_dma, key, extra_deps=()):
        op = Op()
        op.eng = eng; op.fn = fn; op.is_dma = is_dma; op.key = key
        op.needed = False; op.ms = None; op.idx = len(self.ops)
        deps = {}

        def add_dep(d, cnt=None):
            if d is None or d is op:
                return
            if d.is_dma:
                c = self.dma_issued[d.key] if cnt is None else cnt
                k = ("dma", d.key)
                if k not in deps or deps[k][1] < c:
                    deps[k] = (d, c)
            else:
                k = ("eng", d.eng)
                if k not in deps or deps[k][0].idx < d.idx:
                    deps[k] = (d, None)

        for r in reads:
            w = r.w
            if w is not None:
                if not (w.eng == eng == "pe" and not w.is_dma and not is_dma):
                    add_dep(w)
        for r in writes:
            w = r.w
            if w is not None and (w.is_dma or is_dma or w.eng != eng):
                add_dep(w)
            for rd in r.rs:
                if rd.is_dma or is_dma or rd.eng != eng:
                    add_dep(rd)
        for d in extra_deps:
            add_dep(d)
        if is_dma:
            self.dma_issued[key] = self.dma_issued.get(key, 0) + 1
        op.deps = list(deps.values())
        for d, _ in op.deps:
            d.needed = True
        for r in reads:
            r.rs.append(op)
        for r in writes:
            r.w = op
            r.rs = []
        self.ops.append(op)
        if fn is not None and not is_dma:
            self.last[eng] = op
        return op

    def op(self, eng, fn, reads=(), writes=()):
        return self._add(eng, fn, reads, writes, False, None)

    def dma(self, eng, key, fn, reads=(), writes=(), inc=16):
        self.dma_inc[key] = inc
        return self._add(eng, fn, reads, writes, True, key)

    def barrier(self):
        lasts = [o for o in self.last.values()]
        dmas = {}
        for o in self.ops:
            if o.is_dma:
                dmas[o.key] = o
        for e in COMPUTE + ("sp",):
            self._add(e, None, (), (), False, None, extra_deps=lasts + list(dmas.values()))

    def emit(self):
        nc = self.nc
        cnt = {e: 0 for e in COMPUTE}
        for op in self.ops:
            if not op.is_dma and op.needed:
                cnt[op.eng] += 1
                op.ms = cnt[op.eng]
        LIM = 30000
        stack = contextlib.ExitStack()
        with stack:
            esems = {}
            for e in COMPUTE:
                n = cnt[e] // LIM + 1
                esems[e] = [stack.enter_context(nc.semaphore(f"s_{e}_{i}")) for i in range(n)]
            dsems = {}
            for k in self.dma_issued:
                dsems[k] = stack.enter_context(nc.semaphore(f"d_{k}"))
            block = stack.enter_context(nc.Block())

            def stream(ename):
                def body(eng_obj):
                    seen = {}
                    for op in self.ops:
                        if op.eng != ename:
                            continue
                        waits = {}
                        for d, c in op.deps:
                            if d.is_dma:
                                sem = dsems[d.key]; val = self.dma_inc[d.key] * c
                            else:
                                sem = esems[d.eng][(d.ms - 1) // LIM]; val = (d.ms - 1) % LIM + 1
                            if waits.get(sem, 0) < val:
                                waits[sem] = val
                        for sem, val in waits.items():
                            if seen.get(sem, 0) >= val:
                                continue
                            seen[sem] = val
                            eng_obj.wait_ge(sem, val)
                        if op.fn is None:
                            continue
                        ins = op.fn(eng_obj)
                        if op.is_dma:
                            ins.then_inc(dsems[op.key], self.dma_inc[op.key])
                        elif op.ms is not None:
                            ins.then_inc(esems[op.eng][(op.ms - 1) // LIM], 1)
                    if ename == "sp":
                        for k, n in self.dma_issued.items():
                            if seen.get(dsems[k], 0) < self.dma_inc[k] * n:
                                eng_obj.wait_ge(dsems[k], self.dma_inc[k] * n)
                return body

            block.tensor(stream("pe"))
            block.scalar(stream("act"))
            block.vector(stream("dve"))
            block.gpsimd(stream("pool"))
            block.sync(stream("sp"))


class Arena:
    def __init__(self, ap_f32, nbytes):
        self.ap = ap_f32
        self.nbytes = nbytes
        self.off = 0

    def reset(self, off=0):
        self.off = off

    def take(self, shape, dtype):
        n = 1
        for s_ in shape:
            n *= s_
        nb = n * (2 if dtype is BF16 else 4)
        off = self.off
        self.off += (nb + 63) // 64 * 64
        assert self.off <= self.nbytes, ("arena overflow", self.off, self.nbytes)
        v = self.ap[:, off // 4:(off + nb) // 4]
        if dtype is BF16:
            v = v.bitcast(BF16)
        if len(shape) == 2:
            v = v.rearrange("p (a b) -> p a b", a=shape[0])
        elif len(shape) == 3:
            v = v.rearrange("p (a b c) -> p a b c", a=shape[0], b=shape[1])
        return v


ARENA_BYTES = 124 * 1024


class Ctx:
    pass


def build(mode, stop=99):
    nc = bass.Bass("TRN2", target_bir_lowering=False)
    P = Prog(nc)
    has0 = mode in ("L0", "FUSED")
    has1 = mode in ("L1", "FUSED")

    def din(name, shape):
        return nc.dram_tensor(name, list(shape), F32, kind="ExternalInput").ap()

    c = Ctx()
    c.nc = nc; c.P = P
    xall = din("xall", [D, S])
    xown = din("xown", [D, 4 * TC])
    consts_d = din("consts", [128, 384])
    gains_d = din("gains", [128, NG])
    mask_d = {}; wo_d = {}; wgu_d = {}; wd_d = {}
    for l in ([0] if has0 else []) + ([1] if has1 else []):
        mask_d[l] = din(f"mask{l}", [128, 2 * 8 * TC])
        wo_d[l] = din(f"wo{l}", [D, D])
        wgu_d[l] = din(f"wgu{l}", [D, 2 * DFF])
        wd_d[l] = din(f"wd{l}", [DFF, D])
    if has0:
        ropeC = din("ropeC", [128, S]); ropeS = din("ropeS", [128, S])
        ropeCo = din("ropeCo", [128, 4 * TC]); ropeSo = din("ropeSo", [128, 4 * TC])
        lamv_d = din("lamv", [128, 256])
        wqkv_d = din("wqkv", [D, 3 * D])
    if has1:
        kvw_d = din("kvw", [D, 2 * D])
        wq_d = din("wq", [D, D])
    y = nc.dram_tensor("y", [D, 4 * TC], F32, kind="ExternalOutput").ap()
    if mode == "FUSED":
        X1o = [nc.dram_tensor(f"x1o{s}", [D, TC], F32).ap() for s in range(4)]
        X1g = [nc.dram_tensor(f"x1g{s}", [2 * D, TC], F32).ap() for s in range(4)]
        rX1o = [Res() for _ in range(4)]; rX1g = [Res() for _ in range(4)]

    sk = dict(kind="ExternalOutput") if DBG.get("dump") else {}
    Ksc = nc.dram_tensor("Ksc", [8, 128, S], BF16, **sk).ap()
    Vsc = nc.dram_tensor("Vsc", [S, D], BF16, **sk).ap()
    Qsc = nc.dram_tensor("Qsc", [8, 128, 4 * TC], BF16, **sk).ap()
    WGUb = nc.dram_tensor("WGUb", [2, 6, 128, 8, 512], BF16).ap()
    WDb = nc.dram_tensor("WDb", [4, 128, NJ, 256], BF16).ap()
    rWGUb = [Res() for _ in range(48)]
    rWDb = [Res() for _ in range(NJ)]
    rK = [[Res() for _ in range(NCH)] for _ in range(8)]
    rV = [Res() for _ in range(NCH)]
    rQ = [[Res() for _ in range(4)] for _ in range(8)]
    CH2RS = {}
    for r_ in range(2):
        for s_, ch_ in enumerate(OWN[r_]):
            CH2RS[ch_] = (r_, s_)

    Rt = nc.alloc_sbuf_tensor("R", [128, 8 * 4 * TC], F32)
    R3 = Rt[:, :].rearrange("p (a b) -> p a b", a=8)
    rR = [Res() for _ in range(4)]
    arena_t = nc.alloc_sbuf_tensor("arena", [128, ARENA_BYTES // 4], F32)
    A = Arena(arena_t[:, :], ARENA_BYTES)
    gains = nc.alloc_sbuf_tensor("gains_sb", [128, NG], F32)[:, :]
    rG = Res()
    cst = nc.alloc_sbuf_tensor("cst", [128, 384], BF16)[:, :]
    rC = Res()
    ones = cst[:, 0:128]; tri = cst[:, 128:256]; pm = cst[:, 256:384]
    small = nc.alloc_sbuf_tensor("small", [128, 512], F32)[:, :]
    pst = nc.alloc_psum_tensor("ps", [128, 8 * 512], F32)[:, :]
    rPS = [Res() for _ in range(8)]

    def bank(i, n=1):
        return pst[:, i * 512:(i + n) * 512]

    def mm(out, lhsT, rhs, start, stop, R, W):
        P.op("pe", lambda e: e.matmul(out, lhsT=lhsT, rhs=rhs, start=start, stop=stop), R, W)

    def act(out, in_, func, R, W, scale=1.0, bias=0.0):
        P.op("act", lambda e: e.activation(out=out, in_=in_, func=func, bias=bias, scale=scale), R, W)

    def tt(eng, out, in0, in1, op, R, W):
        P.op(eng, lambda e: e.tensor_tensor(out=out, in0=in0, in1=in1, op=op), R, W)

    def ts(eng, out, in0, s1, s2, op0, op1, R, W):
        if op1 is None:
            P.op(eng, lambda e: e.tensor_scalar(out=out, in0=in0, scalar1=s1, scalar2=None, op0=op0), R, W)
        else:
            P.op(eng, lambda e: e.tensor_scalar(out=out, in0=in0, scalar1=s1, scalar2=s2, op0=op0, op1=op1), R, W)

    def stt(eng, out, in0, scalar, in1, op0, op1, R, W):
        P.op(eng, lambda e: e.scalar_tensor_tensor(out=out, in0=in0, scalar=scalar, in1=in1, op0=op0, op1=op1), R, W)

    def cp(eng, out, in_, R, W):
        if eng == "act":
            P.op("act", lambda e: e.copy(out=out, in_=in_), R, W)
        else:
            P.op(eng, lambda e: e.tensor_copy(out=out, in_=in_), R, W)

    def dma(eng, key, out, in_, R, W):
        P.dma(eng, key, lambda e: e.dma_start(out=out, in_=in_), R, W)

    def kcview(ap2d):
        return ap2d.rearrange("(kc p) n -> p kc n", p=128)

    dma("pool", "cst", cst, consts_d, [], [rC])
    dma("sp", "gains", gains, gains_d, [], [rG])

    def norm_stats(xv, xres, sqv, sqres, psn, tln, rstd, rtmp, rrstd, nk=8, scale=1.0 / D):
        act(sqv, xv, AF.Square, [xres], [sqres])
        for kc in range(nk):
            mm(bank(psn), ones, sqv[:, kc, :], kc == 0, kc == nk - 1, [sqres, rC], [rPS[psn]])
        act(tln, bank(psn), AF.Ln, [rPS[psn]], [rtmp], scale=scale, bias=EPS)
        act(rstd, tln, AF.Exp, [rtmp], [rrstd], scale=-0.5)

    def phase_proj(layer):
        A.reset()
        L0 = layer == 0
        if L0:
            WA = A.take([8, 3 * D], BF16)
            wsrc = kcview(wqkv_d)
            qoff, koff, voff = 0, D, 2 * D
        else:
            WA = A.take([8, 3 * D], BF16)
            qoff, koff, voff = 0, D, 2 * D
        rWK, rWV, rWQ = Res(), Res(), Res()
        rWKq = [Res() for _ in range(4)]
        if L0:
            for qq in range(4):
                dma("pool", "wk", WA[:, :, koff + qq * 256:koff + (qq + 1) * 256],
                    wsrc[:, :, D + qq * 256:D + (qq + 1) * 256], [], [rWKq[qq]])
            dma("pool", "wv", WA[:, :, voff:voff + D], wsrc[:, :, 2 * D:3 * D], [], [rWV])
            dma("pool", "wq", WA[:, :, qoff:qoff + D], wsrc[:, :, 0:D], [], [rWQ])
        else:
            kvs = kcview(kvw_d)
            for qq in range(4):
                dma("pool", "wk", WA[:, :, koff + qq * 256:koff + (qq + 1) * 256],
                    kvs[:, :, qq * 256:(qq + 1) * 256], [], [rWKq[qq]])
            dma("pool", "wv", WA[:, :, voff:voff + D], kvs[:, :, D:2 * D], [], [rWV])
            dma("pool", "wq", WA[:, :, qoff:qoff + D], kcview(wq_d), [], [rWQ])
        XC = A.take([8, TC], F32); rXC = Res()
        HT = A.take([8, TC], BF16); rHT = Res()
        SQ = A.take([8, TC], BF16); rSQ = Res()
        TLN = A.take([1, TC], F32)[:, 0, :]; rTLN = Res()
        RSTD = A.take([1, TC], F32)[:, 0, :]; rRSTD = Res()
        RCb = [A.take([1, TC], F32)[:, 0, :] for _ in range(2)]; rRC = [Res(), Res()]
        RSb = [A.take([1, TC], F32)[:, 0, :] for _ in range(2)]; rRS = [Res(), Res()]
        KB = [A.take([1, TC], BF16)[:, 0, :] for _ in range(2)]; rKB = [Res(), Res()]
        T1 = [A.take([1, TC], F32)[:, 0, :] for _ in range(2)]; rT1 = [Res(), Res()]
        T2 = [A.take([1, TC], F32)[:, 0, :] for _ in range(2)]; rT2 = [Res(), Res()]
        KR = [A.take([1, TC], BF16)[:, 0, :] for _ in range(3)]; rKR = [Res(), Res(), Res()]
        VS = [A.take([1, D], BF16)[:, 0, :] for _ in range(2)]; rVS = [Res(), Res()]
        PSN, PSK, PSP, PSV = 0, (1, 2), (3, 4), (5, 6)
        cnt = {"k": 0, "v": 0}
        def norm_chunk(xv, xres, gcol):
            norm_stats(xv, xres, SQ, rSQ, PSN, TLN, RSTD, rTLN, rRSTD)
            for kc in range(8):
                stt("dve", HT[:, kc, :], xv[:, kc, :], gains[:, gcol + kc:gcol + kc + 1], RSTD,
                    ALU.mult, ALU.mult, [xres, rG, rRSTD], [rHT])

        def proj_fm(wres, woff, hh, dst_dram, dst_res, rc, rs_, rcres, rsres):
            i = cnt["k"]; cnt["k"] += 1
            b = i % 2
            for kc in range(8):
                mm(bank(PSK[b]), WA[:, kc, woff + hh * 128:woff + (hh + 1) * 128], HT[:, kc, :],
                   kc == 0, kc == 7, [wres, rHT], [rPS[PSK[b]]])
            kr = i % 3
            if L0:
                cp("act", KB[b], bank(PSK[b]), [rPS[PSK[b]]], [rKB[b]])

            def epilogue():
                if L0:
                    mm(bank(PSP[b]), pm, KB[b], True, True, [rC, rKB[b]], [rPS[PSP[b]]])
                    tt("dve", T1[b], bank(PSK[b]), rc, ALU.mult, [rPS[PSK[b]], rcres], [rT1[b], rPS[PSK[b]]])
                    tt("dve", T2[b], bank(PSP[b]), rs_, ALU.mult, [rPS[PSP[b]], rsres], [rT2[b]])
                    tt("pool", KR[kr], T1[b], T2[b], ALU.add, [rT1[b], rT2[b]], [rKR[kr]])
                else:
                    cp("act", KR[kr], bank(PSK[b]), [rPS[PSK[b]]], [rKR[kr]])
                dma("sp", f"kst{kr}", dst_dram, KR[kr], [rKR[kr]], [dst_res])
            return epilogue

        def proj_heads(kind, i, rb, heads):
            prev = None
            for hh in heads:
                if kind == "kv":
                    ep = proj_fm(rWKq[hh // 2], koff, hh, Ksc[hh, :, i * TC:(i + 1) * TC], rK[hh][i],
                                 RCb[rb], RSb[rb], rRC[rb], rRS[rb])
                else:
                    ep = proj_fm(rWQ, qoff, hh, Qsc[hh, :, i * TC:(i + 1) * TC], rQ[hh][i],
                                 RCb[rb], RSb[rb], rRC[rb], rRS[rb])
                if prev is not None:
                    prev()
                prev = ep
            prev()

        xall_v = kcview(xall)
        xown_v = kcview(xown)
        items = [("kv", ci) for ci in range(NCH)] + [("q", s) for s in range(4)]
        fused_l1 = (mode == "FUSED" and not L0)

        def xview(t):
            kind, i = items[t]
            if kind == "kv":
                return XC, rXC, (G_MIXPRE if L0 else G_KV)
            return R3[:, :, i * TC:(i + 1) * TC], rR[i], G_MIXPRE + 8 * layer

        def prep(t):
            kind, i = items[t]
            rb = t % 2
            if kind == "kv":
                if fused_l1:
                    r_, s_ = CH2RS[i]
                    src = X1g[s_].rearrange("(r kc p) t -> r p kc t", r=2, p=128)[r_]
                    dma("sp", "xc", XC, src, [rX1g[s_]], [rXC])
                else:
                    dma("sp", "xc", XC, xall_v[:, :, i * TC:(i + 1) * TC], [], [rXC])
                if L0:
                    dma("sp", f"rc{rb}", RCb[rb], ropeC[:, i * TC:(i + 1) * TC], [], [rRC[rb]])
                    dma("sp", f"rs{rb}", RSb[rb], ropeS[:, i * TC:(i + 1) * TC], [], [rRS[rb]])
            else:
                if not fused_l1:
                    dma("sp", "xr", R3[:, :, i * TC:(i + 1) * TC], xown_v[:, :, i * TC:(i + 1) * TC], [], [rR[i]])
                if L0:
                    dma("sp", f"rc{rb}", RCb[rb], ropeCo[:, i * TC:(i + 1) * TC], [], [rRC[rb]])
                    dma("sp", f"rs{rb}", RSb[rb], ropeSo[:, i * TC:(i + 1) * TC], [], [rRS[rb]])

        def stats(t):
            xv, xres, gcol = xview(t)
            norm_stats(xv, xres, SQ, rSQ, PSN, TLN, RSTD, rTLN, rRSTD)

        def scale(t):
            xv, xres, gcol = xview(t)
            for kc in range(8):
                stt("dve", HT[:, kc, :], xv[:, kc, :], gains[:, gcol + kc:gcol + kc + 1], RSTD,
                    ALU.mult, ALU.mult, [xres, rG, rRSTD], [rHT])

        def part1(t):
            kind, i = items[t]
            proj_heads(kind, i, t % 2, range(8 if kind == "kv" else 4))

        def part2(t):
            kind, i = items[t]
            rb = t % 2
            if kind == "q":
                proj_heads(kind, i, rb, range(4, 8))
                return
            ci = i
            for tb in range(4):
                vb = cnt["v"] % 2; cnt["v"] += 1
                for half in range(2):
                    pb = PSV[half]
                    for kc in range(8):
                        mm(bank(pb), HT[:, kc, tb * 128:(tb + 1) * 128],
                           WA[:, kc, voff + half * 512:voff + (half + 1) * 512],
                           kc == 0, kc == 7, [rHT, rWV], [rPS[pb]])
                    cp("act" if half == 0 else "dve", VS[vb][:, half * 512:(half + 1) * 512], bank(pb),
                       [rPS[pb]], [rVS[vb]])
                dma("sp", f"vst{vb}", Vsc[ci * TC + tb * 128:ci * TC + (tb + 1) * 128, :], VS[vb],
                    [rVS[vb]], [rV[ci]])

        prep(0); stats(0); scale(0)
        for t in range(len(items)):
            if t + 1 < len(items):
                prep(t + 1)
            part1(t)
            if t + 1 < len(items):
                stats(t + 1)
            part2(t)
            if t + 1 < len(items):
                scale(t + 1)

    AO_BYTES = 8 * 4 * TC * 2

    def make_caster(layer):
        STG = [A.take([1, 1024], BF16)[:, 0, :] for _ in range(2)]; rSTG = [Res(), Res()]
        wgu_src = wgu_d[layer]; wd_src = wd_d[layer]
        jobs = [("gu", kc, gu, gp) for kc in range(8) for gu in range(2) for gp in range(3)]
        jobs += [("d", j, 0, 0) for j in range(NJ)]
        pend = []
        st = {"n": 0}

        def issue(job, sb_):
            kind, a_, gu, gp = job
            if kind == "gu":
                w = 1024 if gp < 2 else 768
                c0 = gu * DFF + gp * 1024
                dma("pool", f"stg{sb_}", STG[sb_][:, 0:w], wgu_src[a_ * 128:(a_ + 1) * 128, c0:c0 + w],
                    [], [rSTG[sb_]])
            else:
                dma("pool", f"stg{sb_}", STG[sb_], wd_src[a_ * 128:(a_ + 1) * 128, :], [], [rSTG[sb_]])

        def store(job, sb_):
            kind, a_, gu, gp = job
            if kind == "gu":
                kc = a_
                rr = rWGUb[(kc * 2 + gu) * 3 + gp]
                if gp < 2:
                    dma("sp", f"stgs{sb_}", WGUb[gu, 2 * gp:2 * gp + 2, :, kc, :].rearrange("g p c -> p g c"),
                        STG[sb_].rearrange("p (g c) -> p g c", g=2), [rSTG[sb_]], [rr])
                else:
                    dma("sp", f"stgs{sb_}", WGUb[gu, 4, :, kc, :], STG[sb_][:, 0:512], [rSTG[sb_]], [rr])
                    dma("sp", f"stgs{sb_}", WGUb[gu, 5, :, kc, 0:256], STG[sb_][:, 512:768], [rSTG[sb_]], [rr])
            else:
                j = a_
                dma("sp", f"stgs{sb_}", WDb[:, :, j, :].rearrange("q p c -> p q c"),
                    STG[sb_].rearrange("p (q c) -> p q c", q=4), [rSTG[sb_]], [rWDb[j]])

        def step(flush=False):
            if pend:
                pj, psb = pend.pop(0)
                store(pj, psb)
            if jobs and not flush:
                sb_ = st["n"] % 2; st["n"] += 1
                job = jobs.pop(0)
                issue(job, sb_)
                pend.append((job, sb_))
            return bool(jobs) or bool(pend)
        return step

    def load_masks(layer):
        MASK = A.take([2, 8, TC], BF16); rM = Res()
        dma("pool", "mask", MASK.rearrange("p a b c -> p (a b c)"), mask_d[layer], [], [rM])
        return MASK, rM

    def load_head(hh, KH, VH, QH, rKH, rVH, rQH):
        b = hh % 2
        dma("sp", f"kh{b}", KH[b], Ksc[hh], [rK[hh][ci] for ci in range(NCH)], [rKH[b]])
        dma("sp", f"vh{b}", VH[b], Vsc[:, hh * 128:(hh + 1) * 128].rearrange("(kb p) d -> p kb d", p=128),
            [rV[ci] for ci in range(NCH)], [rVH[b]])
        dma("sp", f"qh{b}", QH[b], Qsc[hh], [rQ[hh][s] for s in range(4)], [rQH[b]])

    def phase_attn0():
        A.reset()
        AO = A.take([8, 4 * TC], BF16)
        c.AO = AO; c.rAO = [[Res() for _ in range(4)] for _ in range(8)]
        MASK, rM = load_masks(0)
        KH = [A.take([1, S], BF16)[:, 0, :] for _ in range(2)]; rKH = [Res(), Res()]
        VH = [A.take([32, 128], BF16) for _ in range(2)]; rVH = [Res(), Res()]
        QH = [A.take([1, 4 * TC], BF16)[:, 0, :] for _ in range(2)]; rQH = [Res(), Res()]
        NE = 3
        E = [A.take([2, TC], BF16) for _ in range(NE)]; rEc = [[Res(), Res()] for _ in range(NE)]
        f32t = lambda: A.take([1, TC], F32)[:, 0, :]
        R1, R2, T1, T2, OO, TLN, RSTD = [f32t() for _ in range(7)]
        rR1, rR2, rT1, rT2, rOO, rTLN, rRSTD = [Res() for _ in range(7)]
        SQo = A.take([1, TC], BF16); rSQo = Res()
        caster = make_caster(0)
        lamv = small[:, 0:256]; rL = Res()
        dma("sp", "lamv", lamv, lamv_d, [], [rL])
        pr = small[:, 256:384]; sm = small[:, 384:400]; rSM = Res()
        tt("dve", pr[:, 0:64], lamv[:, 0:64], lamv[:, 64:128], ALU.mult, [rL], [rSM])
        tt("dve", pr[:, 64:128], lamv[:, 128:192], lamv[:, 192:256], ALU.mult, [rL], [rSM])
        P.op("dve", lambda e: e.tensor_reduce(out=sm[:, 0:1], in_=pr[:, 0:64], axis=AX.X, op=ALU.add), [rSM], [rSM])
        P.op("dve", lambda e: e.tensor_reduce(out=sm[:, 1:2], in_=pr[:, 64:128], axis=AX.X, op=ALU.add), [rSM], [rSM])
        act(sm[:, 2:4], sm[:, 0:2], AF.Exp, [rSM], [rSM])
        tt("dve", sm[:, 4:5], sm[:, 3:4], sm[:, 2:3], ALU.subtract, [rSM], [rSM])
        ts("dve", sm[:, 5:6], sm[:, 4:5], -LAMBDA_INIT, None, ALU.add, None, [rSM], [rSM])
        ts("dve", sm[:, 6:7], gains[:, G_SUB:G_SUB + 1], 1.0 - LAMBDA_INIT, None, ALU.mult, None, [rSM, rG], [rSM])
        neglam = sm[:, 5:6]; gsub = sm[:, 6:7]
        O1, O2, L1, L2 = 4, 5, 6, 7
        steps = [(hh, s, kb) for hh in range(8) for s in range(4) for kb in range(8 * (s + 1))]
        n = len(steps)

        def emit_S(i):
            hh, s, kb = steps[i]
            hb = hh % 2; sb = i % 2
            qs = slice(s * TC, (s + 1) * TC); ks = slice(kb * 128, (kb + 1) * 128)
            mm(bank(2 * sb), KH[hb][0:64, ks], QH[hb][0:64, qs], True, True,
               [rKH[hb], rQH[hb]], [rPS[2 * sb]])
            mm(bank(2 * sb + 1), KH[hb][64:128, ks], QH[hb][64:128, qs], True, True,
               [rKH[hb], rQH[hb]], [rPS[2 * sb + 1]])

        def emit_rest(i):
            hh, s, kb = steps[i]
            hb = hh % 2; sb = i % 2; eb = i % NE
            nkb = 8 * (s + 1)
            qs = slice(s * TC, (s + 1) * TC)
            if s == 0 and kb == 0 and hh + 1 < 8:
                load_head(hh + 1, KH, VH, QH, rKH, rVH, rQH)
            for cc in range(2):
                act(E[eb][:, cc, :], bank(2 * sb + cc), AF.Exp, [rPS[2 * sb + cc]], [rEc[eb][cc]], scale=0.125)
                if kb >= nkb - 8:
                    mk = MASK[:, s % 2, kb - (nkb - 8), :]
                    tt("dve", E[eb][:, cc, :], E[eb][:, cc, :], mk, ALU.mult, [rEc[eb][cc], rM], [rEc[eb][cc]])
            st, sp_ = kb == 0, kb == nkb - 1
            mm(bank(O1), VH[hb][:, kb, :], E[eb][:, 0, :], st, sp_, [rVH[hb], rEc[eb][0]], [rPS[O1]])
            mm(bank(L1), ones, E[eb][:, 0, :], st, sp_, [rC, rEc[eb][0]], [rPS[L1]])
            mm(bank(O2), VH[hb][:, kb, :], E[eb][:, 1, :], st, sp_, [rVH[hb], rEc[eb][1]], [rPS[O2]])
            mm(bank(L2), ones, E[eb][:, 1, :], st, sp_, [rC, rEc[eb][1]], [rPS[L2]])
            if kb == nkb - 1:
                P.op("dve", lambda e: e.reciprocal(out=R1, in_=bank(L1)), [rPS[L1]], [rR1])
                P.op("dve", lambda e: e.reciprocal(out=R2, in_=bank(L2)), [rPS[L2]], [rR2])
                tt("dve", T1, bank(O1), R1, ALU.mult, [rPS[O1], rR1], [rT1])
                tt("dve", T2, bank(O2), R2, ALU.mult, [rPS[O2], rR2], [rT2])
                stt("dve", OO, T2, neglam, T1, ALU.mult, ALU.add, [rT1, rT2, rSM], [rOO])
                act(SQo[:, 0, :], OO, AF.Square, [rOO], [rSQo])
                pn = 2 * (i % 2)
                mm(bank(pn), ones, SQo[:, 0, :], True, True, [rC, rSQo], [rPS[pn]])
                act(TLN, bank(pn), AF.Ln, [rPS[pn]], [rTLN], scale=1.0 / 128, bias=EPS)
                act(RSTD, TLN, AF.Exp, [rTLN], [rRSTD], scale=-0.5)
                stt("dve", AO[:, hh, qs], OO, gsub, RSTD, ALU.mult, ALU.mult, [rOO, rSM, rRSTD], [c.rAO[hh][s]])

        load_head(0, KH, VH, QH, rKH, rVH, rQH)
        emit_S(0)
        for i in range(n):
            if i + 1 < n:
                emit_S(i + 1)
            emit_rest(i)
            if i % 8 == 4:
                caster()
        while caster():
            pass

    def phase_attn1():
        A.reset()
        AO = A.take([8, 4 * TC], BF16)
        c.AO = AO; c.rAO = [[Res() for _ in range(4)] for _ in range(8)]
        MASK, rM = load_masks(1)
        KH = [A.take([1, S], BF16)[:, 0, :] for _ in range(2)]; rKH = [Res(), Res()]
        VH = [A.take([32, 128], BF16) for _ in range(2)]; rVH = [Res(), Res()]
        QH = [A.take([1, 4 * TC], BF16)[:, 0, :] for _ in range(2)]; rQH = [Res(), Res()]
        EE = [A.take([2, TC], F32) for _ in range(3)]; rEE = [Res() for _ in range(3)]
        SP = [A.take([2, TC], BF16) for _ in range(3)]; rSP = [Res() for _ in range(3)]
        CAR = [A.take([2, TC], BF16) for _ in range(3)]; rCAR = [Res() for _ in range(3)]
        E2 = [A.take([2, TC], BF16) for _ in range(2)]; rE2 = [Res() for _ in range(2)]
        AA = [A.take([2, TC], BF16) for _ in range(2)]; rAA = [Res() for _ in range(2)]
        caster = make_caster(1)
        CB, OB = 4, 6
        fl = lambda v: v.rearrange("p a b -> p (a b)")
        steps = [(hp, s, it, 8 * (s + 1) - 1 - it) for hp in range(8) for s in range(4) for it in range(8 * (s + 1))]
        n = len(steps)

        def info(j):
            hp, s, it, kb = steps[j]
            return hp, s, it, kb, hp % 2, 8 * (s + 1), slice(s * TC, (s + 1) * TC)

        def op_Z(j):
            hp, s, it, kb, hb, nkb, qs = info(j)
            sb = j % 2
            ks = slice(kb * 128, (kb + 1) * 128)
            for u in range(2):
                us = slice(u * 64, (u + 1) * 64)
                mm(bank(2 * sb + u), KH[hb][us, ks], QH[hb][us, qs], True, True,
                   [rKH[hb], rQH[hb]], [rPS[2 * sb + u]])

        def op_expZ(j):
            sb = j % 2; e3 = j % 3
            act(fl(EE[e3]), bank(2 * sb, 2), AF.Exp, [rPS[2 * sb], rPS[2 * sb + 1]], [rEE[e3]], scale=0.125)

        def op_mask(j):
            hp, s, it, kb, hb, nkb, qs = info(j)
            e3 = j % 3
            if kb >= nkb - 8:
                mk = MASK[:, s % 2, kb - (nkb - 8), :]
                for u in range(2):
                    tt("dve", EE[e3][:, u, :], EE[e3][:, u, :], mk, ALU.mult, [rEE[e3], rM], [rEE[e3]])

        def op_ln(j):
            e3 = j % 3
            act(fl(SP[e3]), fl(EE[e3]), AF.Ln, [rEE[e3]], [rSP[e3]], bias=1.0)

        def op_carry(j):
            hp, s, it, kb, hb, nkb, qs = info(j)
            e3 = j % 3
            if it >= 1 and it < nkb - 1:
                if it == 1:
                    tt("dve", fl(CAR[(j + 1) % 3]), fl(SP[(j - 1) % 3]), fl(SP[e3]), ALU.add,
                       [rSP[(j - 1) % 3], rSP[e3]], [rCAR[(j + 1) % 3]])
                else:
                    tt("dve", fl(CAR[(j + 1) % 3]), fl(CAR[j % 3]), fl(SP[e3]), ALU.add,
                       [rCAR[j % 3], rSP[e3]], [rCAR[(j + 1) % 3]])

        def op_C(j):
            hp, s, it, kb, hb, nkb, qs = info(j)
            e3 = j % 3
            for u in range(2):
                if it == 1:
                    mm(bank(CB + u), ones, SP[(j - 1) % 3][:, u, :], True, False, [rC, rSP[(j - 1) % 3]], [rPS[CB + u]])
                elif it >= 2:
                    mm(bank(CB + u), ones, CAR[j % 3][:, u, :], True, False, [rC, rCAR[j % 3]], [rPS[CB + u]])
            for u in range(2):
                mm(bank(CB + u), tri, SP[e3][:, u, :], it == 0, True, [rC, rSP[e3]], [rPS[CB + u]])

        def op_expC(j):
            act(fl(E2[j % 2]), bank(CB, 2), AF.Exp, [rPS[CB], rPS[CB + 1]], [rE2[j % 2]], scale=-1.0)

        def op_a_av(j):
            hp, s, it, kb, hb, nkb, qs = info(j)
            e3 = j % 3
            if s == 0 and it == 0 and hp + 1 < 8:
                load_head(hp + 1, KH, VH, QH, rKH, rVH, rQH)
            tt("dve", fl(AA[j % 2]), fl(EE[e3]), fl(E2[j % 2]), ALU.mult, [rEE[e3], rE2[j % 2]], [rAA[j % 2]])
            for u in range(2):
                mm(bank(OB + u)[0:64, :], VH[hb][:, kb, u * 64:(u + 1) * 64], AA[j % 2][:, u, :],
                   it == 0, it == nkb - 1, [rVH[hb], rAA[j % 2]], [rPS[OB + u]])
            if it == nkb - 1:
                for u in range(2):
                    cp("act" if u == 0 else "dve", AO[u * 64:(u + 1) * 64, hp, qs],
                       bank(OB + u)[0:64, :], [rPS[OB + u]], [c.rAO[hp][s]])

        load_head(0, KH, VH, QH, rKH, rVH, rQH)
        op_Z(0); op_expZ(0); op_mask(0); op_ln(0); op_Z(1)
        for i in range(n + 1):
            if i < n:
                op_carry(i)
                op_C(i)
            if i + 1 < n:
                op_expZ(i + 1)
            if i + 2 < n:
                op_Z(i + 2)
            if i >= 1:
                op_a_av(i - 1)
            if i < n:
                op_expC(i)
            if i + 1 < n:
                op_mask(i + 1)
                op_ln(i + 1)
            if i % 8 == 4:
                caster()
        while caster():
            pass

    def post_norm_add(s, MT, rMT, gcol, SQ, rSQ, psn, TLN, RSTD, rTLN, rRSTD, TMP, rTMP):
        norm_stats(MT, rMT, SQ, rSQ, psn, TLN, RSTD, rTLN, rRSTD)
        for mc in range(8):
            b = mc % 2
            stt("dve", TMP[b], MT[:, mc, :], gains[:, gcol + mc:gcol + mc + 1], RSTD, ALU.mult, ALU.mult,
                [rMT, rG, rRSTD], [rTMP[b]])
            tt("pool", R3[:, mc, s * TC:(s + 1) * TC], R3[:, mc, s * TC:(s + 1) * TC], TMP[b], ALU.add,
               [rR[s], rTMP[b]], [rR[s]])

    def phase_wo(layer):
        A.reset(AO_BYTES)
        WB = A.take([8, D], BF16); rWB = Res()
        dma("pool", "wo", WB, kcview(wo_d[layer]), [], [rWB])
        MT = A.take([8, TC], F32); rMT = Res()
        SQ = A.take([8, TC], BF16); rSQ = Res()
        TLN = A.take([1, TC], F32)[:, 0, :]; rTLN = Res()
        RSTD = A.take([1, TC], F32)[:, 0, :]; rRSTD = Res()
        TMP = [A.take([1, TC], F32)[:, 0, :] for _ in range(2)]; rTMP = [Res(), Res()]
        PSM, PSN = (0, 1), 2
        for s in range(4):
            qs = slice(s * TC, (s + 1) * TC)
            for mc in range(8):
                b = PSM[mc % 2]
                for hc in range(8):
                    mm(bank(b), WB[:, hc, mc * 128:(mc + 1) * 128], c.AO[:, hc, qs], hc == 0, hc == 7,
                       [rWB, c.rAO[hc][s]], [rPS[b]])
                cp("act" if mc % 2 == 0 else "dve", MT[:, mc, :], bank(b), [rPS[b]], [rMT])
            post_norm_add(s, MT, rMT, G_MIXPOST + 8 * layer, SQ, rSQ, PSN, TLN, RSTD, rTLN, rRSTD, TMP, rTMP)

    def phase_ffn(layer, final):
        wgu_ap, wd_ap = wgu_d[layer], wd_d[layer]
        A.reset()
        HT = A.take([8, TC], BF16); rHT = Res()
        SQ = A.take([8, TC], BF16); rSQ = Res()
        TLN = A.take([1, TC], F32)[:, 0, :]; rTLN = Res()
        RSTD = A.take([1, TC], F32)[:, 0, :]; rRSTD = Res()
        WG = [A.take([8, 512], BF16) for _ in range(2)]; rWG = [Res(), Res()]
        WU = [A.take([8, 512], BF16) for _ in range(2)]; rWU = [Res(), Res()]
        WD = [A.take([NJ, 256], BF16) for _ in range(2)]; rWD = [Res(), Res()]
        ACTB = A.take([NJ, TC], BF16); rACT = [Res() for _ in range(NJ)]
        SG = [A.take([1, TC], F32)[:, 0, :] for _ in range(2)]; rSG = [Res(), Res()]
        FT = A.take([8, TC], F32); rFT = Res()
        TMP = [A.take([1, TC], F32)[:, 0, :] for _ in range(2)]; rTMP = [Res(), Res()]
        wgu_v = kcview(wgu_ap)
        wd_v = wd_ap.rearrange("(j p) n -> p j n", p=128)
        PG, PU, PD, PSN = (0, 1), (2, 3), (4, 5), 6
        groups = [(0, 4), (4, 4), (8, 4), (12, 4), (16, 4), (20, 2)]
        cnt = {"g": 0, "q": 0}
        gcol = G_FFNPRE + 8 * layer
        pcol = G_FFNPOST + 8 * layer

        def prenorm(s):
            xv = R3[:, :, s * TC:(s + 1) * TC]
            norm_stats(xv, rR[s], SQ, rSQ, PSN, TLN, RSTD, rTLN, rRSTD)
            for kc in range(8):
                stt("dve", HT[:, kc, :], xv[:, kc, :], gains[:, gcol + kc:gcol + kc + 1], RSTD,
                    ALU.mult, ALU.mult, [rR[s], rG, rRSTD], [rHT])

        def gateup(s, hooks):
            for (j0, ng) in groups:
                wb = cnt["g"] % 2; cnt["g"] += 1
                g_ = j0 // 4
                dma("sp", f"wg{wb}", WG[wb][:, :, 0:ng * 128], WGUb[0, g_, :, :, 0:ng * 128], rWGUb, [rWG[wb]])
                dma("sp", f"wu{wb}", WU[wb][:, :, 0:ng * 128], WGUb[1, g_, :, :, 0:ng * 128], rWGUb, [rWU[wb]])
                for jj in range(ng):
                    j = j0 + jj; b = j % 2
                    for kc in range(8):
                        mm(bank(PG[b]), WG[wb][:, kc, jj * 128:(jj + 1) * 128], HT[:, kc, :], kc == 0, kc == 7,
                           [rWG[wb], rHT], [rPS[PG[b]]])
                    for kc in range(8):
                        mm(bank(PU[b]), WU[wb][:, kc, jj * 128:(jj + 1) * 128], HT[:, kc, :], kc == 0, kc == 7,
                           [rWU[wb], rHT], [rPS[PU[b]]])
                    act(SG[b], bank(PG[b]), AF.Silu, [rPS[PG[b]]], [rSG[b]])
                    tt("dve", ACTB[:, j, :], SG[b], bank(PU[b]), ALU.mult, [rSG[b], rPS[PU[b]]], [rACT[j]])
                if hooks:
                    hooks.pop(0)()
            while hooks:
                hooks.pop(0)()

        def down(s):
            for q in range(4):
                wb = cnt["q"] % 2; cnt["q"] += 1
                dma("sp", f"wd{wb}", WD[wb], WDb[q], rWDb, [rWD[wb]])
                for m2 in range(2):
                    mc = q * 2 + m2; b = PD[mc % 2]
                    for j in range(NJ):
                        mm(bank(b), WD[wb][:, j, m2 * 128:(m2 + 1) * 128], ACTB[:, j, :], j == 0, j == NJ - 1,
                           [rWD[wb], rACT[j]], [rPS[b]])
                    cp("dve" if mc % 2 == 0 else "act", FT[:, mc, :], bank(b), [rPS[b]], [rFT])

        def post_pieces(s):
            qs = slice(s * TC, (s + 1) * TC)

            def p1():
                act(SQ, FT, AF.Square, [rFT], [rSQ])

            def p2():
                for kc in range(8):
                    mm(bank(PSN), ones, SQ[:, kc, :], kc == 0, kc == 7, [rSQ, rC], [rPS[PSN]])
                act(TLN, bank(PSN), AF.Ln, [rPS[PSN]], [rTLN], scale=1.0 / D, bias=EPS)
                act(RSTD, TLN, AF.Exp, [rTLN], [rRSTD], scale=-0.5)

            def p3():
                for mc in range(8):
                    b = mc % 2
                    stt("dve", TMP[b], FT[:, mc, :], gains[:, pcol + mc:pcol + mc + 1], RSTD, ALU.mult, ALU.mult,
                        [rFT, rG, rRSTD], [rTMP[b]])
                    tt("pool", R3[:, mc, qs], R3[:, mc, qs], TMP[b], ALU.add, [rR[s], rTMP[b]], [rR[s]])
                if final:
                    dma("pool", "yst", kcview(y)[:, :, qs], R3[:, :, qs], [rR[s]], [])
                else:
                    dma("pool", "x1st", kcview(X1o[s]), R3[:, :, qs], [rR[s]], [rX1o[s]])
                    P.dma("pool", "cc", lambda e: e.collective_compute(
                        "AllGather", ALU.bypass, replica_groups=[[0, 1], [2, 3], [4, 5], [6, 7]],
                        ins=[X1o[s]], outs=[X1g[s]]), [rX1o[s]], [rX1g[s]], inc=1)
            return [p1, p2, p3]

        prenorm(0)
        hooks = []
        for s in range(4):
            gateup(s, hooks)
            if s + 1 < 4:
                prenorm(s + 1)
            down(s)
            hooks = post_pieces(s)
        for h in hooks:
            h()

    layers = [0] if mode == "L0" else [1] if mode == "L1" else [0, 1]
    for li, layer in enumerate(layers):
        last = li == len(layers) - 1
        phase_proj(layer); P.barrier()
        (phase_attn0 if layer == 0 else phase_attn1)(); P.barrier()
        phase_wo(layer); P.barrier()
        phase_ffn(layer, last); P.barrier()
    P.emit()
    return nc


def _consts():
    k = np.arange(128)
    ones = np.ones((128, 128), np.float32)
    tri = (k[:, None] >= k[None, :]).astype(np.float32)
    pm = np.zeros((128, 128), np.float32)
    for m in range(128):
        d = m % 64
        if d < 8:
            pm[m + 8, m] = -1.0
        elif d < 16:
            pm[m - 8, m] = 1.0
    return np.concatenate([ones, tri, pm], axis=1)


def _rope_tables():
    pos = np.arange(S, dtype=np.float32)
    inv = (np.float32(500000.0) ** (-np.arange(0, 16, 2, dtype=np.float32) / np.float32(16))).astype(np.float32)
    ang = (pos[:, None] * inv[None, :]).astype(np.float32)
    cos = np.cos(ang).astype(np.float32); sin = np.sin(ang).astype(np.float32)
    C = np.ones((128, S), np.float32); Sn = np.zeros((128, S), np.float32)
    for p in range(128):
        d = p % 64
        if d < 16:
            C[p] = cos[:, d % 8]; Sn[p] = sin[:, d % 8]
    return C, Sn


def _masks(r, strict):
    k = np.arange(128)[:, None]; q = np.arange(TC)[None, :]
    out = np.zeros((128, 2, 8, TC), np.float32)
    for par in range(2):
        own = OWN[r][par]
        hi = (own % 2 == 1)
        for p in range(8):
            lhs = 128 * p + k - (512 if hi else 0)
            out[:, par, p, :] = (lhs < q) if strict else (lhs <= q)
    return out.reshape(128, 2 * 8 * TC)


def _pack_gains(inp):
    g = np.zeros((128, NG), np.float32)

    def put(col, vec):
        g[:, col:col + 8] = np.asarray(vec, np.float32).reshape(8, 128).T

    for l in range(2):
        put(G_MIXPRE + 8 * l, inp["mix_pre_g"][l]); put(G_MIXPOST + 8 * l, inp["mix_post_g"][l])
        put(G_FFNPRE + 8 * l, inp["ffn_pre_g"][l]); put(G_FFNPOST + 8 * l, inp["ffn_post_g"][l])
    put(G_KV, inp["kv_norm_g"])
    g[:, G_SUB] = np.asarray(inp["a_subln_g"], np.float32).reshape(128)
    return g


_NC_CACHE = {}


def _get_nc(mode):
    if mode not in _NC_CACHE:
        _NC_CACHE[mode] = build(mode)
    return _NC_CACHE[mode]


def _own_cols(xT, r):
    return np.ascontiguousarray(np.concatenate([xT[:, ch * TC:(ch + 1) * TC] for ch in OWN[r]], axis=1))


def _scatter(ys):
    out = np.zeros((4, S, D), np.float32)
    for cidx in range(8):
        b, r = cidx // 2, cidx % 2
        yT = ys[cidx]
        for s_, ch in enumerate(OWN[r]):
            out[b, ch * TC:(ch + 1) * TC, :] = yT[:, s_ * TC:(s_ + 1) * TC].T
    return out


def make_maps(inp, mode, x_in=None):
    f = lambda a: np.ascontiguousarray(np.asarray(a, np.float32))
    consts = _consts()
    gains = _pack_gains(inp)
    x = f(inp["x"]) if x_in is None else x_in
    has0 = mode in ("L0", "FUSED"); has1 = mode in ("L1", "FUSED")
    if has0:
        C, Sn = _rope_tables()
        lamv = np.ascontiguousarray(np.concatenate(
            [np.broadcast_to(np.asarray(inp[k], np.float32).reshape(1, 64), (128, 64))
             for k in ("a_lambda_q1", "a_lambda_k1", "a_lambda_q2", "a_lambda_k2")], axis=1))
    maps = []
    for cidx in range(8):
        b, r = cidx // 2, cidx % 2
        xT = np.ascontiguousarray(x[b].T)
        m = {"xall": xT, "xown": _own_cols(xT, r), "consts": consts, "gains": gains}
        if has0:
            m.update({"mask0": _masks(r, False), "wo0": f(inp["a_w_o"][0]), "wgu0": f(inp["ffn_w_gate_up"][0]),
                      "wd0": f(inp["ffn_w_down"][0]), "ropeC": C, "ropeS": Sn, "ropeCo": _own_cols(C, r),
                      "ropeSo": _own_cols(Sn, r), "lamv": lamv, "wqkv": f(inp["a_w_qkv"][0])})
        if has1:
            m.update({"mask1": _masks(r, True), "wo1": f(inp["b_w_o"][0]), "wgu1": f(inp["ffn_w_gate_up"][1]),
                      "wd1": f(inp["ffn_w_down"][1]), "kvw": f(inp["kv_w"]), "wq": f(inp["b_w_q"][0])})
        maps.append(m)
    return maps


def kernel_unfused(**inp):
    inp = {k: np.asarray(v) for k, v in inp.items()}
    res = run_bass_kernel_spmd(_get_nc("L0"), make_maps(inp, "L0"), core_ids=list(range(8)))
    x1 = _scatter([res.results[i]["y"] for i in range(8)])
    res = run_bass_kernel_spmd(_get_nc("L1"), make_maps(inp, "L1", x1), core_ids=list(range(8)))
    return _scatter([res.results[i]["y"] for i in range(8)])


def kernel(**inp):
    inp = {k: np.asarray(v) for k, v in inp.items()}
    res = run_bass_kernel_spmd(_get_nc("FUSED"), make_maps(inp, "FUSED"), core_ids=list(range(8)))
    return _scatter([res.results[i]["y"] for i in range(8)])
```
